# Optimizing a Trainium2 kernel written in Bass

```python
import math
import jax, jax.numpy as jnp
from jax import lax
import numpy as np

D_MODEL = 1024
BATCH = 8
SEQ = 2048
DEPTH = 1

HEAD_DIM = 64
N_Q_HEADS = 8
N_KV_HEADS = 2
GROUP = N_Q_HEADS // N_KV_HEADS
WINDOW = 128
BLOCK = 128
ATTN_WIDTH = N_Q_HEADS * HEAD_DIM
KV_WIDTH = N_KV_HEADS * HEAD_DIM
CONV_CHANNELS = 512
CONV_GROUPS = 8
CONV_WIDTH = 31
N_BRANCHES = 2
Q_OFF = 0
K_OFF = Q_OFF + ATTN_WIDTH
V_OFF = K_OFF + KV_WIDTH
GLU_OFF = V_OFF + KV_WIDTH
GATE_OFF = GLU_OFF + 2 * CONV_CHANNELS
IN_WIDTH = GATE_OFF + N_BRANCHES * D_MODEL
D_FF = int(math.ceil(8 * D_MODEL / 3 / 256)) * 256
EPS = 1e-5
NEG = -1e30

kernel_name = "hybrid_swa_sink_conformer_conv_gated"


def rmsnorm(x, g):
    xf = x.astype(jnp.float32)
    y = xf * lax.rsqrt(jnp.mean(xf * xf, axis=-1, keepdims=True) + EPS)
    return (y * g.astype(jnp.float32)).astype(x.dtype)


def layernorm(x, g, b):
    xf = x.astype(jnp.float32)
    mu = jnp.mean(xf, axis=-1, keepdims=True)
    xc = xf - mu
    var = jnp.mean(xc * xc, axis=-1, keepdims=True)
    y = xc * lax.rsqrt(var + EPS) * g.astype(jnp.float32) + b.astype(jnp.float32)
    return y.astype(x.dtype)


def sliding_window_attention(q, k, v, sinks):
    B, S = q.shape[0], q.shape[1]
    nb = S // BLOCK
    qb = q.reshape(B, nb, BLOCK, N_KV_HEADS, GROUP, HEAD_DIM)

    def band(t):
        padded = jnp.pad(t, ((0, 0), (BLOCK, 0), (0, 0), (0, 0)))
        prev = padded[:, :S].reshape(B, nb, BLOCK, N_KV_HEADS, HEAD_DIM)
        cur = t.reshape(B, nb, BLOCK, N_KV_HEADS, HEAD_DIM)
        return jnp.concatenate([prev, cur], axis=2)

    kb = band(k)
    vb = band(v)
    scale = HEAD_DIM ** -0.5
    s = jnp.einsum('bnqhgd,bnkhd->bnhgqk', qb, kb).astype(jnp.float32) * scale
    qi = jnp.arange(BLOCK)[:, None]
    kj = jnp.arange(2 * BLOCK)[None, :]
    diff = qi + BLOCK - kj
    kpos = jnp.arange(nb)[:, None, None] * BLOCK - BLOCK + kj[None]
    valid = (diff >= 0)[None] & (diff < WINDOW)[None] & (kpos >= 0)
    s = jnp.where(valid[None, :, None, None], s, NEG)
    sink_col = jnp.broadcast_to(
        sinks.astype(jnp.float32).reshape(1, 1, N_KV_HEADS, GROUP, 1, 1),
        s.shape[:-1] + (1,))
    p = jax.nn.softmax(jnp.concatenate([s, sink_col], axis=-1), axis=-1)[..., :-1]
    o = jnp.einsum('bnhgqk,bnkhd->bnqhgd', p.astype(v.dtype), vb)
    return o.reshape(B, S, ATTN_WIDTH)


def conformer_conv(u, conv_w, conv_b, ln_g, ln_b):
    a, b = jnp.split(u, 2, axis=-1)
    z = a * jax.nn.sigmoid(b)
    z = lax.conv_general_dilated(
        z, conv_w[:, None, :].astype(z.dtype),
        window_strides=(1,), padding=[(CONV_WIDTH - 1, 0)],
        dimension_numbers=('NWC', 'WIO', 'NWC'),
        feature_group_count=CONV_CHANNELS) + conv_b
    z = layernorm(z, ln_g, ln_b)
    return jax.nn.silu(z)


def setup_inputs(seed: int = 0) -> dict:
    key = jax.random.key(seed)
    ks = jax.random.split(key, 20)
    L, D, C = DEPTH, D_MODEL, CONV_CHANNELS
    nrm = lambda k, shape, fan_in: jax.random.normal(k, shape, jnp.float32) * fan_in ** -0.5
    gain = lambda k, shape: 1.0 + 0.01 * jax.random.normal(k, shape, jnp.float32)
    small = lambda k, shape: 0.01 * jax.random.normal(k, shape, jnp.float32)
    return {
        "x": jax.random.normal(ks[0], (BATCH, SEQ, D), jnp.float32),
        "g_mix_norm": gain(ks[1], (L, D)),
        "w_in": nrm(ks[2], (L, D, IN_WIDTH), D),
        "b_in": small(ks[3], (L, IN_WIDTH)),
        "sinks": 0.5 * jax.random.normal(ks[4], (L, N_Q_HEADS), jnp.float32),
        "conv_w": nrm(ks[5], (L, CONV_WIDTH, C), CONV_WIDTH),
        "conv_b": small(ks[6], (L, C)),
        "ln_g": gain(ks[7], (L, C)),
        "ln_b": small(ks[8], (L, C)),
        "w_attn_proj": nrm(ks[9], (L, ATTN_WIDTH, D), ATTN_WIDTH),
        "w_conv_proj": nrm(ks[10], (L, C, D), C),
        "b_conv_proj": small(ks[11], (L, D)),
        "w_out": nrm(ks[12], (L, D, D), D),
        "g_ffn_norm": gain(ks[13], (L, D)),
        "w_ffn_in": nrm(ks[14], (L, D, 2 * D_FF), D),
        "w_ffn_down": nrm(ks[15], (L, D_FF, D), D_FF),
        "g_final": gain(ks[16], (D,)),
    }


def reference(x, g_mix_norm, w_in, b_in, sinks, conv_w, conv_b, ln_g, ln_b,
              w_attn_proj, w_conv_proj, b_conv_proj, w_out, g_ffn_norm,
              w_ffn_in, w_ffn_down, g_final):
    B, S, D = x.shape
    for l in range(DEPTH):
        h = rmsnorm(x, g_mix_norm[l])
        proj = h @ w_in[l] + b_in[l]
        q = proj[..., Q_OFF:K_OFF].reshape(B, S, N_Q_HEADS, HEAD_DIM)
        k = proj[..., K_OFF:V_OFF].reshape(B, S, N_KV_HEADS, HEAD_DIM)
        v = proj[..., V_OFF:GLU_OFF].reshape(B, S, N_KV_HEADS, HEAD_DIM)
        glu_in = proj[..., GLU_OFF:GATE_OFF]
        gates = jax.nn.sigmoid(proj[..., GATE_OFF:].reshape(B, S, N_BRANCHES, D))

        y_attn = sliding_window_attention(q, k, v, sinks[l]) @ w_attn_proj[l]
        y_conv = conformer_conv(glu_in, conv_w[l], conv_b[l], ln_g[l], ln_b[l]) @ w_conv_proj[l] + b_conv_proj[l]
        merged = gates[:, :, 0] * y_attn + gates[:, :, 1] * y_conv
        x = x + merged @ w_out[l]

        h2 = rmsnorm(x, g_ffn_norm[l])
        gu = h2 @ w_ffn_in[l]
        gate, up = gu[..., :D_FF], gu[..., D_FF:]
        x = x + (jax.nn.silu(gate) * up) @ w_ffn_down[l]
    return rmsnorm(x, g_final)
```

```python
from contextlib import ExitStack

import numpy as np
import concourse.bass as bass
import concourse.mybir as mybir
from concourse.bass_utils import run_bass_kernel_spmd

F32 = mybir.dt.float32
BF16 = mybir.dt.bfloat16
I32 = mybir.dt.int32
AF = mybir.ActivationFunctionType
ALU = mybir.AluOpType

NCORES = 8
D = 1024
T = 2048
NTT = 16
NTG = 4
HD = 64
NQH = 8
ATTN_W = 512
KV_W = 128
CC = 512
CW = 31
Q_OFF = 0
K_OFF = 512
V_OFF = 640
GLU_OFF = 768
GATE_OFF = 1792
IN_W = 3840
DFF = 2816
NFF = 22
EPS = 1e-5

ENGS = ("pe", "act", "dve", "pool", "sp")
NDMA_SEM = 16


class Op:
    __slots__ = ("eng", "fn", "dma", "deps", "signal", "ticket", "dsem",
                 "dticket", "idx", "gidx")

    def __init__(self, eng, fn, dma):
        self.eng = eng
        self.fn = fn
        self.dma = dma
        self.deps = []
        self.signal = False
        self.ticket = 0
        self.dsem = 0
        self.dticket = 0


class Sched:
    def __init__(self):
        self.ops = {e: [] for e in ENGS}
        self.lastw = {}
        self.readers = {}
        self.dma_hist = {e: [] for e in ENGS}
        self.n = 0

    @staticmethod
    def _rkey(op):
        return ("d", op.gidx) if op.dma else ("c", op.eng)

    def add(self, eng, fn, r=(), w=(), dma=False):
        op = Op(eng, fn, dma)
        op.gidx = self.n
        self.n += 1
        op.idx = len(self.ops[eng])
        deps = {}
        for x in r:
            d = self.lastw.get(x)
            if d is not None:
                deps[d] = True
        for x in w:
            d = self.lastw.get(x)
            if d is not None:
                deps.setdefault(d, False)
            for rd in self.readers.get(x, {}).values():
                deps.setdefault(rd, False)
        for d, raw in deps.items():
            if d is op:
                continue
            if (not dma) and (not d.dma) and d.eng == eng:
                if eng == "pe":
                    continue
            op.deps.append(d)
            d.signal = True
        if dma:
            hist = self.dma_hist[eng]
            k = len(hist)
            op.dsem = k % NDMA_SEM
            op.dticket = 16 * (k // NDMA_SEM + 1)
            if k >= NDMA_SEM:
                op.deps.append(hist[k - NDMA_SEM])
            hist.append(op)
        for x in r:
            self.readers.setdefault(x, {})[self._rkey(op)] = op
        for x in w:
            self.lastw[x] = op
            self.readers[x] = {}
        self.ops[eng].append(op)
        return op

    def alias(self, news, olds):
        if isinstance(news, str):
            news = [news]
        for new in news:
            self._alias1(new, olds)

    def _alias1(self, new, olds):
        rd = self.readers.setdefault(new, {})
        for o in olds:
            cands = list(self.readers.get(o, {}).values())
            lw = self.lastw.get(o)
            if lw is not None:
                cands.append(lw)
            for c in cands:
                k = self._rkey(c)
                if k not in rd or rd[k].gidx < c.gidx:
                    rd[k] = c

    def emit(self, nc, stack):
        for e in ENGS:
            t = 0
            for op in self.ops[e]:
                if op.dma:
                    continue
                if op.signal:
                    t += 1
                    op.ticket = t
        csem = {e: stack.enter_context(nc.semaphore("c_" + e)) for e in ENGS}
        dsem = {e: [stack.enter_context(nc.semaphore("d_%s_%d" % (e, i)))
                    for i in range(NDMA_SEM)] for e in ("sp", "pool", "act")}
        block = stack.enter_context(nc.Block())

        def run(ename):
            def body(eng):
                known = {}
                for op in self.ops[ename]:
                    waits = {}
                    for d in op.deps:
                        if d.dma:
                            key, val = ("d", d.eng, d.dsem), d.dticket
                        else:
                            key, val = ("c", d.eng), d.ticket
                        if known.get(key, 0) >= val:
                            continue
                        if waits.get(key, 0) < val:
                            waits[key] = val
                    for key, val in waits.items():
                        sem = csem[key[1]] if key[0] == "c" else dsem[key[1]][key[2]]
                        eng.wait_ge(sem, val)
                        known[key] = val
                    ins = op.fn(eng)
                    if ins is None:
                        continue
                    if op.dma:
                        ins.then_inc(dsem[ename][op.dsem], 16)
                    elif op.signal:
                        ins.then_inc(csem[ename], 1)
            return body

        block.tensor(run("pe"))
        block.scalar(run("act"))
        block.vector(run("dve"))
        block.gpsimd(run("pool"))
        block.sync(run("sp"))


class Arena:
    def __init__(self, t, nbytes):
        self.t = t
        self.nbytes = nbytes

    def f32(self, off, n):
        assert off % 4 == 0 and off + 4 * n <= self.nbytes, (off, n)
        return self.t[:, off // 4: off // 4 + n]

    def bf16(self, off, n):
        assert off % 4 == 0 and n % 2 == 0 and off + 2 * n <= self.nbytes, (off, n)
        return self.t[:, off // 4: off // 4 + n // 2].bitcast(BF16)

    def i32(self, off, n):
        return self.t[:, off // 4: off // 4 + n].bitcast(I32)


def build_program(stop_after="all", dumps=()):
    nc = bass.Bass("TRN2", target_bir_lowering=False)
    S = Sched()
    stack = ExitStack()

    def din(name, shape):
        return nc.dram_tensor(name, list(shape), F32, kind="ExternalInput").ap()

    x_d = din("x", [T, D])
    g_mix_d = din("g_mix_norm", [1, D])
    w_in_d = din("w_in", [D, IN_W])
    b_in_d = din("b_in", [1, IN_W])
    sinks_d = din("sinks", [1, NQH])
    conv_w_d = din("conv_w", [CW, CC])
    conv_b_d = din("conv_b", [1, CC])
    ln_g_d = din("ln_g", [1, CC])
    ln_b_d = din("ln_b", [1, CC])
    w_ap_d = din("w_attn_proj", [ATTN_W, D])
    w_cp_d = din("w_conv_proj", [CC, D])
    b_cp_d = din("b_conv_proj", [1, D])
    w_out_d = din("w_out", [D, D])
    g_ffn_d = din("g_ffn_norm", [1, D])
    w_fi_d = din("w_ffn_in", [D, 2 * DFF])
    w_fd_d = din("w_ffn_down", [DFF, D])
    g_fin_d = din("g_final", [1, D])
    out_d = nc.dram_tensor("out", [T, D], F32, kind="ExternalOutput").ap()

    ARENA_BYTES = 211968
    arena_t = stack.enter_context(nc.sbuf_tensor("arena", [128, ARENA_BYTES // 4], F32))
    A = Arena(arena_t, ARENA_BYTES)
    off = 0

    def region(nbytes):
        nonlocal off
        o = off
        off += (nbytes + 31) // 32 * 32
        assert off <= ARENA_BYTES, off
        return o

    X_OFF = region(65536)
    GBC_OFF = region(4096)
    IDENT_OFF = region(256)
    MASKC_OFF = region(256)
    MASKP_OFF = region(256)
    ONES_OFF = region(256)
    MB_OFF = region(2048)
    BIN_OFF = region(30 * 4)
    BQP_OFF = region(4 * 4)
    BVBC_OFF = region(128 * 4)
    CONVW_OFF = region(4 * CW * 4)
    CVEC_OFF = region(3 * 4 * 4)
    BCP_OFF = region(8 * 4)
    SINK_OFF = region(8 * 4)
    SS_OFF = region(48 * 4)
    RSTD_OFF = region(48 * 4)
    IOTA_OFF = region(128 * 4)
    EPSB_OFF = region(32)
    EM_OFF = region(64)
    WCOL_OFF = region(16 * 8 * 4)
    H_OFF = region(32768)
    M_OFF = region(41280)
    WOUT_OFF = region(16384)
    TMP_OFF = region(28736)
    NRING = 8
    RING_OFF = region(2048 * NRING)

    x_tm = A.f32(X_OFF, 16 * 1024).rearrange("p (n d) -> p n d", n=16)
    attnT = A.bf16(X_OFF, 4 * T).rearrange("p (c t) -> p c t", c=4)
    convT = A.bf16(X_OFF + 16384, 4 * T).rearrange("p (c t) -> p c t", c=4)
    wS = A.bf16(X_OFF + 32768, 16 * 8 * 32).rearrange("p (g m c) -> p g m c", g=16, m=8)
    ZSW = 544
    zs = [A.bf16(X_OFF + 40960 + 8704 * i, 2 * 4 * ZSW).rearrange("p (h g t) -> p h g t", h=2, g=4)
          for i in range(2)]
    NXS = 8
    xstage = A.f32(X_OFF + 32768, NXS * 1024).rearrange("p (s d) -> p s d", s=NXS)

    gbc = A.f32(GBC_OFF, 1024)
    ident = A.bf16(IDENT_OFF, 128)
    mask_c = A.bf16(MASKC_OFF, 128)
    mask_p = A.bf16(MASKP_OFF, 128)
    ones_bf = A.bf16(ONES_OFF, 128)
    maskb = A.bf16(MB_OFF, 1024).rearrange("p (w h q) -> p w h q", w=2, h=4)
    b_in_fm = A.f32(BIN_OFF, 30)
    bq_perm = A.f32(BQP_OFF, 4)
    bv_bc = A.f32(BVBC_OFF, 128)
    convw = A.f32(CONVW_OFF, 4 * CW).rearrange("p (c j) -> p c j", c=4)
    cvec = A.f32(CVEC_OFF, 12).rearrange("p (v c) -> p v c", v=3)
    bcp = A.f32(BCP_OFF, 8)
    expsink = A.f32(SINK_OFF, 8)
    ss = A.f32(SS_OFF, 48)
    rstd = A.f32(RSTD_OFF, 48)
    iota_i = A.i32(IOTA_OFF, 128)
    epsb = A.f32(EPSB_OFF, 1)
    emask = A.bf16(EM_OFF, 32)
    wcol = A.f32(WCOL_OFF, 128).rearrange("p (g m) -> p g m", g=16)
    hT = A.bf16(H_OFF, 8 * T).rearrange("p (c t) -> p c t", c=8)
    qT = A.bf16(M_OFF, 4 * T).rearrange("p (c t) -> p c t", c=4)
    kT = A.bf16(M_OFF + 16384, T)
    v_aug = A.bf16(M_OFF + 20480, 16 * 2 * 65).rearrange("p (n g d) -> p n g d", n=16, g=2)
    ZPAD = 32
    zT = A.bf16(M_OFF + 24640, 4 * (T + ZPAD)).rearrange("p (c t) -> p c t", c=4)
    mergedT = A.bf16(M_OFF, 8 * T).rearrange("p (c t) -> p c t", c=8)
    w_out_sb = A.bf16(WOUT_OFF, 8 * 1024).rearrange("p (k n) -> p k n", k=8)
    ring = [A.bf16(RING_OFF + 2048 * i, 1024).rearrange("p (k m) -> p k m", k=8)
            for i in range(NRING)]
    NXN = 3
    xn = A.bf16(TMP_OFF, NXN * 1024).rearrange("p (b d) -> p b d", b=NXN)
    NPT = 3
    ptb = A.bf16(TMP_OFF, NPT * 2 * 512).rearrange("p (s w q) -> p s w q", s=NPT, w=2)
    attn_tm = A.bf16(TMP_OFF + 6144, 2 * 512).rearrange("p (b f) -> p b f", b=2)
    den = A.f32(TMP_OFF + 8192, 2 * 8).rearrange("p (b h) -> p b h", b=2)
    y32 = A.f32(TMP_OFF + 8256, 4 * 512).rearrange("p (c t) -> p c t", c=4)
    sig = A.f32(TMP_OFF + 8256, 2 * 512).rearrange("p (b t) -> p b t", b=2)
    ybf = A.bf16(TMP_OFF + 16448, 4 * 512).rearrange("p (c t) -> p c t", c=4)
    ysq = A.bf16(TMP_OFF + 20544, 4 * 512).rearrange("p (c t) -> p c t", c=4)
    w32 = arena_t[0:32, (TMP_OFF + 20544) // 4:(TMP_OFF + 20544) // 4 + 512]
    w32b = arena_t[0:32, (TMP_OFF + 16448) // 4:(TMP_OFF + 16448) // 4 + 256].bitcast(BF16)
    stA = A.f32(TMP_OFF + 24640, 512)
    stB = A.f32(TMP_OFF + 26688, 512)
    etmp = A.f32(TMP_OFF + 8256, 8 * 512).rearrange("p (b k t) -> p b k t", b=2, k=4)
    FF_GROUPS = [6, 6, 5, 5]
    actT = A.bf16(M_OFF, 6 * T).rearrange("p (f t) -> p f t", f=6)
    wd_sb = [A.bf16(M_OFF + 24576 + 12288 * i, 6 * 1024).rearrange("p (f n) -> p f n", f=6)
             for i in range(2)]
    sgt = A.f32(M_OFF + 49152, 2 * 512).rearrange("p (b t) -> p b t", b=2)

    ps = [stack.enter_context(nc.psum_tensor("ps%d" % i, [128, 512], F32)) for i in range(8)]
    ps_bf = [p.bitcast(BF16) for p in ps]
    bank_ctr = [0]

    def next_bank():
        b = bank_ctr[0] % 8
        bank_ctr[0] += 1
        return b

    def dma(eng, out, in_, r=(), w=(), **kw):
        return S.add(eng, lambda e: e.dma_start(out=out, in_=in_, **kw), r=r, w=w, dma=True)

    def op(eng, fn, r=(), w=()):
        return S.add(eng, fn, r=r, w=w)

    def mm(out, lhsT, rhs, start, stop, r, w):
        return op("pe", lambda e: e.matmul(out=out, lhsT=lhsT, rhs=rhs, start=start, stop=stop), r, w)

    def act(out, in_, func, r, w, **kw):
        return op("act", lambda e: e.activation(out=out, in_=in_, func=func, **kw), r, w)

    def tt_(eng, out, in0, in1, alu, r, w):
        return op(eng, lambda e: e.tensor_tensor(out=out, in0=in0, in1=in1, op=alu), r, w)

    def ts_(eng, out, in0, s1, s2, op0, op1, r, w):
        if s2 is None:
            return op(eng, lambda e: e.tensor_scalar(out=out, in0=in0, scalar1=s1, scalar2=None,
                                                     op0=op0), r, w)
        return op(eng, lambda e: e.tensor_scalar(out=out, in0=in0, scalar1=s1, scalar2=s2,
                                                 op0=op0, op1=op1), r, w)

    def stt_(eng, out, in0, scalar, in1, op0, op1, r, w):
        return op(eng, lambda e: e.scalar_tensor_tensor(out=out, in0=in0, scalar=scalar, in1=in1,
                                                        op0=op0, op1=op1), r, w)

    def hres(tg):
        return ["hT.%d" % t for t in range(4 * tg, 4 * tg + 4)]

    dump_list = []
    dumps_avail = {}

    def finish():
        for nm in dumps:
            ap, res = dumps_avail[nm]
            d = nc.dram_tensor("dbg_" + nm, list(ap.shape), ap.dtype, kind="ExternalOutput").ap()
            dma("sp", d, ap, r=res, w=["dbg_" + nm])
            dump_list.append("dbg_" + nm)
        S.add("sp", lambda e: None, r=["out.%d" % t for t in range(NTT)] + dump_list)
        S.emit(nc, stack)
        stack.close()
        return nc

    wtasks = []
    wstate = {"issued": 0, "consumed": 0, "done": 0}

    def w_issue_upto(n):
        while wstate["issued"] < min(n, len(wtasks)):
            i = wstate["issued"]
            wtasks[i](i % NRING)
            wstate["issued"] += 1

    def w_next():
        i = wstate["consumed"]
        wstate["consumed"] += 1
        assert i < wstate["issued"], (i, wstate)
        return i % NRING

    def w_done(n=1):
        wstate["done"] += n
        w_issue_upto(wstate["done"] + NRING)

    def wtask_cols(src, col_blocks):
        def t(r):
            for i_, (m0, c0, wd) in enumerate(col_blocks):
                wl = ["ring%d.%d" % (r, i_)] + (["ring%d.1" % r] if len(col_blocks) == 1 else [])
                dma("pool", ring[r][:, :, m0:m0 + wd],
                    src[:, c0:c0 + wd].rearrange("(k p) m -> p k m", p=128),
                    w=wl)
        return t

    def wtask_proj(c):
        def t(r):
            dma("pool", ring[r][:, 0:4, :],
                w_ap_d[:, c * 128:(c + 1) * 128].rearrange("(k p) m -> p k m", p=128),
                w=["ring%d.0" % r])
            dma("pool", ring[r][:, 4:8, :],
                w_cp_d[:, c * 128:(c + 1) * 128].rearrange("(k p) m -> p k m", p=128),
                w=["ring%d.1" % r])
        return t

    for c in range(4):
        wtasks.append(wtask_cols(w_in_d, [(0, Q_OFF + 64 * c, 64), (64, Q_OFF + 64 * (c + 4), 64)]))
    wtasks.append(wtask_cols(w_in_d, [(0, K_OFF, 128)]))
    wtasks.append(wtask_cols(w_in_d, [(0, V_OFF, 128)]))
    for c in range(4):
        wtasks.append(wtask_cols(w_in_d, [(0, GLU_OFF + 128 * c, 128)]))
        wtasks.append(wtask_cols(w_in_d, [(0, GLU_OFF + CC + 128 * c, 128)]))
    for c in range(8):
        wtasks.append(wtask_proj(c))
        wtasks.append(wtask_cols(w_in_d, [(0, GATE_OFF + 128 * c, 128)]))
        wtasks.append(wtask_cols(w_in_d, [(0, GATE_OFF + D + 128 * c, 128)]))
    for f in range(NFF):
        wtasks.append(wtask_cols(w_fi_d, [(0, 128 * f, 128)]))
        wtasks.append(wtask_cols(w_fi_d, [(0, DFF + 128 * f, 128)]))

    NCD = dict(allow_slow_non_contiguous=True)
    xv = x_d.rearrange("(n p) d -> p n d", p=128)
    dma("sp", gbc, g_mix_d.broadcast_to([128, D]), w=["gbc"])
    for tt in range(NXS):
        dma("sp", xstage[:, tt, :], xv[:, tt, :], w=["xs%d" % tt])
    dma("sp", b_in_fm, b_in_d.rearrange("o (c p) -> p (o c)", p=128), w=["b_in_fm"], **NCD)
    for g in range(2):
        dma("sp", bq_perm[64 * g:64 * (g + 1), :],
            b_in_d[:, 256 * g:256 * (g + 1)].rearrange("o (c p) -> p (o c)", p=64),
            w=["bq_perm%d" % g], **NCD)
    dma("sp", bv_bc, b_in_d[:, V_OFF:V_OFF + 128].broadcast_to([128, 128]), w=["bv_bc"])
    dma("sp", expsink, sinks_d.broadcast_to([128, NQH]), w=["expsink"])

    op("pool", lambda e: e.memset(ss, 0.0), w=["ss"])
    op("pool", lambda e: e.memset(epsb, EPS), w=["epsb"])
    op("pool", lambda e: e.iota(iota_i, [[1, 128]], base=0, channel_multiplier=-1), w=["iota"])
    ts_("pool", ident, iota_i, 0, None, ALU.is_equal, None, r=["iota"], w=["ident"])
    w_issue_upto(NRING)
    ts_("pool", mask_c, iota_i, 0, None, ALU.is_ge, None, r=["iota"], w=["mask_c"])
    ts_("pool", mask_p, iota_i, 0, None, ALU.is_lt, None, r=["iota"], w=["mask_p"])
    op("pool", lambda e: e.memset(ones_bf, 1.0 / CC), w=["ones"])
    op("pool", lambda e: e.memset(w32[0:1, :], 0.0), w=["w32z"])


    def late_consts():
        dma("sp", w32[1:32, :], conv_w_d, r=[], w=["w32"])
        dma("sp", cvec[:, 0, :], conv_b_d.rearrange("o (c p) -> p (o c)", p=128), w=["cvec0"], **NCD)
        dma("sp", cvec[:, 1, :], ln_g_d.rearrange("o (c p) -> p (o c)", p=128), w=["cvec1"], **NCD)
        dma("sp", cvec[:, 2, :], ln_b_d.rearrange("o (c p) -> p (o c)", p=128), w=["cvec2"], **NCD)
        dma("sp", bcp, b_cp_d.rearrange("o (c p) -> p (o c)", p=128), w=["bcp"], **NCD)

    def rms_stage1a(src_tile, tt, si, src_res):
        b = tt % NXN
        col = 16 * si + tt
        sres = "ss%d" % col
        rres = "rstd%d" % col
        act(xn[:, b, :], src_tile, AF.Square, r=[src_res, "ss"], w=["xn%d" % b, sres],
            accum_out=ss[:, col:col + 1])
        act(rstd[:, col:col + 1], ss[:, col:col + 1], AF.Sqrt, r=[sres, "epsb"], w=[rres],
            scale=1.0 / D, bias=epsb)

    def rms_stage1b(src_tile, tt, si, src_res):
        b = tt % NXN
        col = 16 * si + tt
        rres = "rstd%d" % col
        op("dve", lambda e: e.reciprocal(out=rstd[:, col:col + 1], in_=rstd[:, col:col + 1]),
           r=[rres], w=[rres])
        stt_("dve", xn[:, b, :], src_tile, rstd[:, col:col + 1], gbc, ALU.mult, ALU.mult,
             r=[src_res, rres, "gbc"], w=["xn%d" % b])

    def rms_stage1(src_tile, tt, si, src_res):
        rms_stage1a(src_tile, tt, si, src_res)
        rms_stage1b(src_tile, tt, si, src_res)

    def rms_stage2(tt, dstT, dst_pref, copy_eng="act"):
        b = tt % NXN
        bank = next_bank()
        for c in range(8):
            op("pe", lambda e, c=c: e.transpose(out=ps_bf[bank][:, c * 128:(c + 1) * 128],
                                                in_=xn[:, b, c * 128:(c + 1) * 128],
                                                identity=ident),
               r=["xn%d" % b, "ident"], w=["ps%d" % bank])
        if copy_eng == "act":
            act(dstT[:, :, tt * 128:(tt + 1) * 128],
                ps_bf[bank].rearrange("p (c j) -> p c j", c=8), AF.Copy,
                r=["ps%d" % bank], w=["%s.%d" % (dst_pref, tt)])
        else:
            op("dve", lambda e: e.tensor_copy(out=dstT[:, :, tt * 128:(tt + 1) * 128],
                                              in_=ps_bf[bank].rearrange("p (c j) -> p c j", c=8)),
               r=["ps%d" % bank], w=["%s.%d" % (dst_pref, tt)])

    SCALE = HD ** -0.5

    def fm_matmuls(r, tg, bank):
        for k in range(8):
            mm(ps[bank][:, :], ring[r][:, k, :], hT[:, k, tg * 512:(tg + 1) * 512],
               k == 0, k == 7, r=["ring%d.0" % r, "ring%d.1" % r] + hres(tg), w=["ps%d" % bank])

    op("pool", lambda e: e.memset(v_aug[:, :, :, 64:65], 1.0), w=["v_ones"])
    op("pool", lambda e: e.memset(zT[:, :, 0:ZPAD], 0.0), w=["z_pad"])
    rq = [w_next() for _ in range(4)]
    rk = w_next()
    rv = w_next()

    def part1(tg):
        for c in range(4):
            bank = next_bank()
            fm_matmuls(rq[c], tg, bank)
            act(qT[:, c, tg * 512:(tg + 1) * 512], ps[bank][:, :], AF.Identity,
                r=["ps%d" % bank, "bq_perm0", "bq_perm1"], w=["qT.%d" % tg], bias=bq_perm[:, c:c + 1])
        bank = next_bank()
        fm_matmuls(rk, tg, bank)
        act(kT[:, tg * 512:(tg + 1) * 512], ps[bank][:, :], AF.Identity,
            r=["ps%d" % bank, "b_in_fm"], w=["kT.%d" % tg], bias=b_in_fm[:, 4:5])
        bank = next_bank()
        for j in range(4):
            tt = 4 * tg + j
            for k in range(8):
                mm(ps[bank][:, j * 128:(j + 1) * 128], hT[:, k, tt * 128:(tt + 1) * 128],
                   ring[rv][:, k, :], k == 0, k == 7,
                   r=["ring%d.0" % rv, "ring%d.1" % rv, "hT.%d" % tt], w=["ps%d" % bank])
        tt_("dve", v_aug[:, 4 * tg:4 * tg + 4, :, 0:64],
            ps[bank].rearrange("p (n g d) -> p n g d", n=4, g=2),
            bv_bc.rearrange("p (g d) -> p g d", g=2).unsqueeze(1).broadcast_to([128, 4, 2, 64]),
            ALU.add, r=["ps%d" % bank, "bv_bc"], w=["v.%d" % tg])

    for tt in range(NTT):
        st = tt % NXS
        if tt >= NXS:
            dma("sp", xstage[:, st, :], xv[:, tt, :], w=["xs%d" % st])
        if tt == 12:
            late_consts()
        rms_stage1(xstage[:, st, :], tt, 0, "xs%d" % st)
        if tt >= 1:
            rms_stage2(tt - 1, hT, "hT", copy_eng="dve")
        if tt >= 4 and tt % 4 == 1:
            part1(tt // 4 - 1)
    rms_stage2(NTT - 1, hT, "hT", copy_eng="dve")
    part1(3)
    w_done(6)
    act(expsink, expsink, AF.Exp, r=["expsink"], w=["expsink"])
    NEGB = 30000.0
    ts_("dve", maskb[:, 1, :, :], mask_c.unsqueeze(1).broadcast_to([128, 4, 128]), -1.0, NEGB,
        ALU.add, ALU.mult, r=["mask_c"], w=["maskb1"])
    ts_("dve", maskb[:, 0, :, :], mask_p.unsqueeze(1).broadcast_to([128, 4, 128]), -1.0, NEGB,
        ALU.add, ALU.mult, r=["mask_p"], w=["maskb0"])

    op("dve", lambda e: e.tensor_copy(out=w32b, in_=w32), r=["w32", "w32z"], w=["w32b"])
    wbank = next_bank()
    selv = ident[0:32, 0:32].rearrange("p (m q) -> p m q", q=4)
    for G in range(16):
        for jj in range(4):
            op("pe", lambda e, G=G, jj=jj: e.matmul(
                out=ps[wbank][32 * jj:32 * (jj + 1), 8 * G:8 * (G + 1)],
                lhsT=w32b[:, 32 * G:32 * (G + 1)], rhs=selv[:, :, jj],
                start=True, stop=True, tile_position=(0, 32 * jj)),
               r=["w32b", "ident"], w=["ps%d" % wbank])
    op("dve", lambda e: e.tensor_copy(out=wcol, in_=ps[wbank][:, 0:128].rearrange("p (g m) -> p g m", g=16)),
       r=["ps%d" % wbank], w=["wcol"])

    dumps_avail["hT"] = (hT, ["hT.%d" % t for t in range(NTT)])
    if stop_after == "A":
        return finish()

    S.alias(["wS"] + ["zs%d.%d.%d" % (h_, g_, j_) for h_ in range(2) for g_ in range(4) for j_ in range(4)],
            ["xs%d" % i for i in range(NXS)])
    tt_("dve", emask, ident[:, 0:32], ident[:, 32:64], ALU.add, r=["ident"], w=["emask"])
    tt_("dve", emask, emask, ident[:, 64:96], ALU.add, r=["ident", "emask"], w=["emask"])
    tt_("dve", emask, emask, ident[:, 96:128], ALU.add, r=["ident", "emask"], w=["emask"])

    def build_diag(c):
        for G in range(4 * c, 4 * c + 4):
            tt_("dve", wS[:, G, :, :], emask.unsqueeze(1).broadcast_to([128, 8, 32]),
                wcol[:, G, :].unsqueeze(2).broadcast_to([128, 8, 32]), ALU.mult,
                r=["emask", "wcol"], w=["wS"])

    def fill_zs(tg, half):
        t0 = tg * 512 + (ZPAD - 31)
        for cg in range(4):
            for jj in range(4):
                zr = ["z_pad"] + ["zT.%d.%d" % (c_, t_) for c_ in (2 * half, 2 * half + 1)
                                  for t_ in ((tg - 1, tg) if tg > 0 else (tg,))]
                dma("sp" if (cg < 2 or tg == 0) else "pool", zs[half][32 * jj:32 * (jj + 1), :, cg, 0:540],
                    zT[32 * cg:32 * (cg + 1), 2 * half:2 * half + 2, t0 + jj:t0 + jj + 540],
                    r=zr, w=["zs%d.%d.%d" % (half, cg, jj)])

    for c in range(4):
        ra = w_next()
        rb = w_next()
        for tg in range(NTG):
            ba = next_bank()
            fm_matmuls(ra, tg, ba)
            bb = next_bank()
            fm_matmuls(rb, tg, bb)
            sb = (c * NTG + tg) % 2
            act(sig[:, sb, :], ps[bb][:, :], AF.Sigmoid, r=["ps%d" % bb, "b_in_fm"], w=["sig%d" % sb],
                bias=b_in_fm[:, 10 + c:11 + c])
            stt_("dve", zT[:, c, ZPAD + tg * 512:ZPAD + (tg + 1) * 512], ps[ba][:, :],
                 b_in_fm[:, 6 + c:7 + c], sig[:, sb, :], ALU.add, ALU.mult,
                 r=["ps%d" % ba, "sig%d" % sb, "b_in_fm"], w=["zT.%d.%d" % (c, tg)])
        w_done(2)
        build_diag(c)

    dumps_avail["qT"] = (qT, ["qT.%d" % t for t in range(NTG)])
    dumps_avail["kT"] = (kT, ["kT.%d" % t for t in range(NTG)])
    dumps_avail["v_aug"] = (v_aug, ["v.%d" % t for t in range(NTG)] + ["v_ones"])
    dumps_avail["zT"] = (zT, ["zT.%d.%d" % (c, t) for c in range(4) for t in range(NTG)] + ["z_pad"])
    if stop_after == "B":
        return finish()

    for hh in range(2):
        dma("pool", w_out_sb[:, :, hh * 512:(hh + 1) * 512],
            w_out_d[:, hh * 512:(hh + 1) * 512].rearrange("(k p) n -> p k n", p=128),
            w=["w_out%d" % hh])

    S.alias(["pt.%d.%d" % (s_, w_) for s_ in range(NPT) for w_ in range(2)], ["xn0", "xn1", "xn2"])
    S.alias(["y32.%d" % c_ for c_ in range(4)], ["sig0", "sig1"])
    S.alias(["ybf.%d" % c_ for c_ in range(4)] + ["ysq.%d" % c_ for c_ in range(4)], ["w32", "w32z", "w32b"])
    steps = [(n, g) for n in range(NTT) for g in range(2)]

    def emit_scores(si):
        n, g = steps[si]
        s = si % NPT
        rows = slice(64 * g, 64 * g + 64)
        q_rhs = qT[rows, :, n * 128:(n + 1) * 128]
        qres = ["qT.%d" % (n // 4)]
        blocks = [(1, n)] + ([(0, n - 1)] if n > 0 else [])
        for (wsel, kb) in blocks:
            bank = next_bank()
            mm(ps[bank][:, :], kT[rows, kb * 128:(kb + 1) * 128], q_rhs, True, False,
               r=qres + ["kT.%d" % (kb // 4)], w=["ps%d" % bank])
            mm(ps[bank][:, :], ident, maskb[:, wsel, :, :].rearrange("p h q -> p (h q)"), False, True,
               r=["ident", "maskb%d" % wsel], w=["ps%d" % bank])
            pres = "pt.%d.%d" % (s, wsel)
            act(ptb[:, s, wsel, :], ps[bank][:, :], AF.Exp, r=["ps%d" % bank], w=[pres],
                scale=SCALE)

    def emit_pv(si):
        n, g = steps[si]
        s = si % NPT
        b = n % 2
        bank = next_bank()
        for i in range(4):
            o = ps[bank][:, i * 65:(i + 1) * 65]
            if n > 0:
                mm(o, ptb[:, s, 0, i * 128:(i + 1) * 128], v_aug[:, n - 1, g, :], True, False,
                   r=["pt.%d.0" % s, "v.%d" % ((n - 1) // 4), "v_ones"], w=["ps%d" % bank])
            mm(o, ptb[:, s, 1, i * 128:(i + 1) * 128], v_aug[:, n, g, :], n == 0, True,
               r=["pt.%d.1" % s, "v.%d" % (n // 4), "v_ones"], w=["ps%d" % bank])
        pso = ps[bank][:, 0:260].rearrange("p (h d) -> p h d", h=4)
        dres = "den%d.%d" % (b, g)
        tt_("dve", den[:, b, 4 * g:4 * g + 4], pso[:, :, 64], expsink[:, 4 * g:4 * g + 4], ALU.add,
            r=["ps%d" % bank, "expsink"], w=[dres])
        op("dve", lambda e: e.reciprocal(out=den[:, b, 4 * g:4 * g + 4], in_=den[:, b, 4 * g:4 * g + 4]),
           r=[dres], w=[dres])
        tt_("dve", attn_tm[:, b, 256 * g:256 * (g + 1)].rearrange("p (h d) -> p h d", h=4),
            pso[:, :, 0:64],
            den[:, b, 4 * g:4 * g + 4].unsqueeze(2).broadcast_to([128, 4, 64]), ALU.mult,
            r=["ps%d" % bank, dres], w=["attn_tm%d.%d" % (b, g)])

    def emit_attn_T(n):
        b = n % 2
        bank = next_bank()
        for c in range(4):
            op("pe", lambda e, c=c: e.transpose(out=ps_bf[bank][:, c * 128:(c + 1) * 128],
                                                in_=attn_tm[:, b, c * 128:(c + 1) * 128],
                                                identity=ident),
               r=["attn_tm%d.0" % b, "attn_tm%d.1" % b, "ident"], w=["ps%d" % bank])
        act(attnT[:, :, n * 128:(n + 1) * 128],
            ps_bf[bank][:, 0:512].rearrange("p (c j) -> p c j", c=4), AF.Copy,
            r=["ps%d" % bank], w=["attnT.%d" % n])

    conv_units = [(tg, c) for tg in range(NTG) for c in range(4)]

    def emit_conv_unit(ui):
        tg, c = conv_units[ui]
        bank = next_bank()
        half = c // 2
        for m in range(8):
            for cg in range(4):
                op("pe", lambda e, m=m, cg=cg: e.matmul(
                    out=ps[bank][32 * cg:32 * (cg + 1), :], lhsT=wS[:, 4 * c + cg, m, :],
                    rhs=zs[half][:, c % 2, cg, 4 * m:4 * m + 512],
                    start=(m == 0), stop=(m == 7), tile_position=(0, 32 * cg)),
                   r=["wS"] + ["zs%d.%d.%d" % (half, cg, j_) for j_ in range(4)], w=["ps%d" % bank])
        act(y32[:, c, :], ps[bank][:, :], AF.Identity, r=["ps%d" % bank, "cvec0", "cvec1", "cvec2"],
            w=["y32.%d" % c], bias=cvec[:, 0, c:c + 1])
        act(ysq[:, c, :], ps[bank][:, :], AF.Square, r=["ps%d" % bank, "cvec0", "cvec1", "cvec2"],
            w=["ysq.%d" % c], bias=cvec[:, 0, c:c + 1])
        op("dve", lambda e: e.tensor_copy(out=ybf[:, c, :], in_=y32[:, c, :]),
           r=["y32.%d" % c], w=["ybf.%d" % c])

    def ln_a(tg):
        bm = next_bank()
        for c in range(4):
            mm(ps[bm][:, :], ones_bf, ybf[:, c, :], c == 0, c == 3,
               r=["ones", "ybf.%d" % c], w=["ps%d" % bm])
        bq = next_bank()
        for c in range(4):
            mm(ps[bq][:, :], ones_bf, ysq[:, c, :], c == 0, c == 3,
               r=["ones", "ysq.%d" % c], w=["ps%d" % bq])
        act(stA, ps[bm][:, :], AF.Square, r=["ps%d" % bm], w=["stA"])
        tt_("dve", stB, ps[bq][:, :], stA, ALU.subtract, r=["ps%d" % bq, "stA"], w=["stB"])
        act(stB, stB, AF.Ln, r=["stB", "epsb"], w=["stB"], bias=epsb)
        act(stB, stB, AF.Exp, r=["stB"], w=["stB"], scale=-0.5)
        tt_("dve", stA, ps[bm][:, :], stB, ALU.mult, r=["ps%d" % bm, "stB"], w=["stA"])

    def ln_b(tg, c):
        tt_("dve", y32[:, c, :], y32[:, c, :], stB, ALU.mult, r=["y32.%d" % c, "stB"],
            w=["y32.%d" % c])
        tt_("dve", y32[:, c, :], y32[:, c, :], stA, ALU.subtract, r=["y32.%d" % c, "stA"],
            w=["y32.%d" % c])
        act(convT[:, c, tg * 512:(tg + 1) * 512], y32[:, c, :], AF.Silu,
            r=["y32.%d" % c, "cvec0", "cvec1", "cvec2"], w=["convT.%d.%d" % (c, tg)],
            scale=cvec[:, 1, c:c + 1], bias=cvec[:, 2, c:c + 1])

    fill_zs(0, 0)
    fill_zs(0, 1)
    LOOK = NPT - 1
    for si in range(min(LOOK, len(steps))):
        emit_scores(si)
    cu = 0
    for si in range(len(steps)):
        if si + LOOK < len(steps):
            emit_scores(si + LOOK)
        emit_pv(si)
        if steps[si][1] == 1:
            n = steps[si][0]
            if n >= 4 and n % 4 == 0:
                for c_ in range(4):
                    ln_b(n // 4 - 1, c_)
            emit_conv_unit(cu)
            cu += 1
            if n % 4 == 1 and n // 4 + 1 < NTG:
                fill_zs(n // 4 + 1, 0)
            if n % 4 == 3 and n // 4 + 1 < NTG:
                fill_zs(n // 4 + 1, 1)
            if n % 4 == 3:
                ln_a(n // 4)
            if n >= 1:
                emit_attn_T(n - 1)
    emit_attn_T(NTT - 1)
    for c in range(4):
        ln_b(NTG - 1, c)
    assert cu == len(conv_units)

    dumps_avail["attnT"] = (attnT, ["attnT.%d" % n for n in range(NTT)])
    dumps_avail["convT"] = (convT, ["convT.%d.%d" % (c, t) for c in range(4) for t in range(NTG)])
    if stop_after == "D":
        return finish()

    S.alias(["mergedT.%d.%d" % (c_, t_) for c_ in range(8) for t_ in range(NTG)], ["qT.%d" % t for t in range(NTG)] + ["kT.%d" % t for t in range(NTG)]
            + ["v.%d" % t for t in range(NTG)] + ["v_ones", "z_pad"]
            + ["zT.%d.%d" % (c, t) for c in range(4) for t in range(NTG)])
    S.alias(["etmp.%d.%d" % (b_, i_) for b_ in range(2) for i_ in range(4)], ["y32.0", "y32.1", "y32.2", "y32.3", "ybf.0", "ybf.1", "ybf.2", "ybf.3",
                     "ysq.0", "ysq.1", "ysq.2", "ysq.3"])
    attn_res = [["attnT.%d" % n for n in range(4 * tg, 4 * tg + 4)] for tg in range(NTG)]
    ei = 0
    for c in range(8):
        rp = w_next()
        rg0 = w_next()
        rg1 = w_next()
        for tg in range(NTG):
            tsl = slice(tg * 512, (tg + 1) * 512)
            eb = ei % 2
            ei += 1
            bya = next_bank()
            for k in range(4):
                mm(ps[bya][:, :], ring[rp][:, k, :], attnT[:, k, tsl], k == 0, k == 3,
                   r=["ring%d.0" % rp, "ring%d.1" % rp] + attn_res[tg], w=["ps%d" % bya])
            byc = next_bank()
            for k in range(4):
                mm(ps[byc][:, :], ring[rp][:, 4 + k, :], convT[:, k, tsl], k == 0, k == 3,
                   r=["ring%d.0" % rp, "ring%d.1" % rp] + ["convT.%d.%d" % (k, tg)], w=["ps%d" % byc])
            bg0 = next_bank()
            fm_matmuls(rg0, tg, bg0)
            bg1 = next_bank()
            fm_matmuls(rg1, tg, bg1)
            er = ["etmp.%d.%d" % (eb, i) for i in range(4)]
            act(etmp[:, eb, 0, :], ps[bg0][:, :], AF.Sigmoid, r=["ps%d" % bg0, "b_in_fm"],
                w=[er[0]], bias=b_in_fm[:, 14 + c:15 + c])
            act(etmp[:, eb, 1, :], ps[bg1][:, :], AF.Sigmoid, r=["ps%d" % bg1, "b_in_fm"],
                w=[er[1]], bias=b_in_fm[:, 22 + c:23 + c])
            tt_("dve", etmp[:, eb, 2, :], ps[bya][:, :], etmp[:, eb, 0, :], ALU.mult,
                r=["ps%d" % bya, er[0]], w=[er[2]])
            stt_("dve", etmp[:, eb, 3, :], ps[byc][:, :], bcp[:, c:c + 1], etmp[:, eb, 1, :],
                 ALU.add, ALU.mult, r=["ps%d" % byc, er[1], "bcp"], w=[er[3]])
            tt_("dve", mergedT[:, c, tsl], etmp[:, eb, 2, :], etmp[:, eb, 3, :], ALU.add,
                r=[er[2], er[3]], w=["mergedT.%d.%d" % (c, tg)])
        w_done(3)

    dumps_avail["mergedT"] = (mergedT, ["mergedT.%d.%d" % (c, t) for c in range(8) for t in range(NTG)])
    if stop_after == "E":
        return finish()

    S.alias(["x.%d" % t_ for t_ in range(NTT)], ["attnT.%d" % n for n in range(NTT)]
            + ["convT.%d.%d" % (c, t) for c in range(4) for t in range(NTG)] + ["wS"]
            + ["zs%d.%d.%d" % (h_, g_, j_) for h_ in range(2) for g_ in range(4) for j_ in range(4)])
    S.alias("xn0", ["pt.%d.%d" % (s, w) for s in range(NPT) for w in range(2)])
    S.alias("xn1", ["pt.%d.%d" % (s, w) for s in range(NPT) for w in range(2)])
    S.alias("xn2", ["pt.%d.%d" % (s, w) for s in range(NPT) for w in range(2)])
    dma("sp", gbc, g_ffn_d.broadcast_to([128, D]), w=["gbc"])
    for tt in range(NTT):
        dma("sp", x_tm[:, tt, :], xv[:, tt, :], w=["x.%d" % tt])
    for tt in range(NTT):
        for fh in range(2):
            bank = next_bank()
            for k in range(8):
                mm(ps[bank][:, :], mergedT[:, k, tt * 128:(tt + 1) * 128],
                   w_out_sb[:, k, fh * 512:(fh + 1) * 512], k == 0, k == 7,
                   r=["mergedT.%d.%d" % (k, tt // 4), "w_out%d" % fh], w=["ps%d" % bank])
            tt_("dve", x_tm[:, tt, fh * 512:(fh + 1) * 512], x_tm[:, tt, fh * 512:(fh + 1) * 512],
                ps[bank][:, :], ALU.add, r=["x.%d" % tt, "ps%d" % bank], w=["x.%d" % tt])
        rms_stage1a(x_tm[:, tt, :], tt, 1, "x.%d" % tt)
        if tt >= 1:
            rms_stage1b(x_tm[:, tt - 1, :], tt - 1, 1, "x.%d" % (tt - 1))
        if tt >= 2:
            rms_stage2(tt - 2, hT, "hT")
    rms_stage1b(x_tm[:, NTT - 1, :], NTT - 1, 1, "x.%d" % (NTT - 1))
    rms_stage2(NTT - 2, hT, "hT")
    rms_stage2(NTT - 1, hT, "hT")

    dumps_avail["x_tm"] = (x_tm, ["x.%d" % t for t in range(NTT)])
    if stop_after == "F":
        return finish()

    ffn_old = (["w_out0", "w_out1"]
               + ["mergedT.%d.%d" % (c, t) for c in range(8) for t in range(NTG)])
    S.alias(["actT.%d.%d" % (f_, t_) for f_ in range(6) for t_ in range(NTG)]
            + ["wd0", "wd1", "sgt0", "sgt1"], ffn_old)
    ov = out_d.rearrange("(n p) d -> p n d", p=128)

    def final_norm(tt):
        col = 32 + tt
        sres = "ss%d" % col
        rres = "rstd%d" % col
        act(xn[:, 0, :], x_tm[:, tt, :], AF.Square, r=["x.%d" % tt, "ss"], w=["xn0", sres],
            accum_out=ss[:, col:col + 1])
        act(rstd[:, col:col + 1], ss[:, col:col + 1], AF.Sqrt, r=[sres, "epsb"], w=[rres],
            scale=1.0 / D, bias=epsb)
        op("dve", lambda e: e.reciprocal(out=rstd[:, col:col + 1], in_=rstd[:, col:col + 1]),
           r=[rres], w=[rres])
        stt_("dve", x_tm[:, tt, :], x_tm[:, tt, :], rstd[:, col:col + 1], gbc, ALU.mult, ALU.mult,
             r=["x.%d" % tt, rres, "gbc"], w=["x.%d" % tt])
        dma("sp", ov[:, tt, :], x_tm[:, tt, :], r=["x.%d" % tt], w=["out.%d" % tt])

    f0 = 0
    si_ = 0
    for gi, nf in enumerate(FF_GROUPS):
        last = gi == len(FF_GROUPS) - 1
        if last:
            dma("sp", gbc, g_fin_d.broadcast_to([128, D]), w=["gbc"])
        wb = gi % 2
        dma("pool", wd_sb[wb][:, 0:nf, :],
            w_fd_d[f0 * 128:(f0 + nf) * 128, :].rearrange("(f p) n -> p f n", p=128),
            r=[], w=["wd%d" % wb])
        for fi in range(nf):
            rg = w_next()
            ru = w_next()
            for tg in range(NTG):
                bg = next_bank()
                fm_matmuls(rg, tg, bg)
                bu = next_bank()
                fm_matmuls(ru, tg, bu)
                sb = si_ % 2
                si_ += 1
                act(sgt[:, sb, :], ps[bg][:, :], AF.Silu, r=["ps%d" % bg], w=["sgt%d" % sb])
                tt_("dve", actT[:, fi, tg * 512:(tg + 1) * 512], sgt[:, sb, :], ps[bu][:, :], ALU.mult,
                    r=["sgt%d" % sb, "ps%d" % bu], w=["actT.%d.%d" % (fi, tg)])
            w_done(2)
        for tt in range(NTT):
            for fh in range(2):
                bank = next_bank()
                for fi in range(nf):
                    mm(ps[bank][:, :], actT[:, fi, tt * 128:(tt + 1) * 128],
                       wd_sb[wb][:, fi, fh * 512:(fh + 1) * 512], fi == 0, fi == nf - 1,
                       r=["actT.%d.%d" % (fi, tt // 4), "wd%d" % wb], w=["ps%d" % bank])
                tt_("dve", x_tm[:, tt, fh * 512:(fh + 1) * 512], x_tm[:, tt, fh * 512:(fh + 1) * 512],
                    ps[bank][:, :], ALU.add, r=["x.%d" % tt, "ps%d" % bank], w=["x.%d" % tt])
            if last and tt >= 1:
                final_norm(tt - 1)
        f0 += nf
    final_norm(NTT - 1)
    return finish()


_NC_CACHE = {}


def _get_nc():
    if "nc" not in _NC_CACHE:
        _NC_CACHE["nc"] = build_program()
    return _NC_CACHE["nc"]


def make_in_maps(inputs):
    x = np.ascontiguousarray(inputs["x"], dtype=np.float32)
    shared = {
        "g_mix_norm": inputs["g_mix_norm"].reshape(1, D),
        "w_in": inputs["w_in"].reshape(D, IN_W),
        "b_in": inputs["b_in"].reshape(1, IN_W),
        "sinks": inputs["sinks"].reshape(1, NQH),
        "conv_w": inputs["conv_w"].reshape(CW, CC),
        "conv_b": inputs["conv_b"].reshape(1, CC),
        "ln_g": inputs["ln_g"].reshape(1, CC),
        "ln_b": inputs["ln_b"].reshape(1, CC),
        "w_attn_proj": inputs["w_attn_proj"].reshape(ATTN_W, D),
        "w_conv_proj": inputs["w_conv_proj"].reshape(CC, D),
        "b_conv_proj": inputs["b_conv_proj"].reshape(1, D),
        "w_out": inputs["w_out"].reshape(D, D),
        "g_ffn_norm": inputs["g_ffn_norm"].reshape(1, D),
        "w_ffn_in": inputs["w_ffn_in"].reshape(D, 2 * DFF),
        "w_ffn_down": inputs["w_ffn_down"].reshape(DFF, D),
        "g_final": inputs["g_final"].reshape(1, D),
    }
    shared = {k: np.ascontiguousarray(np.asarray(v), dtype=np.float32) for k, v in shared.items()}
    maps = []
    for i in range(NCORES):
        m = dict(shared)
        m["x"] = np.ascontiguousarray(x[i])
        maps.append(m)
    return maps


def kernel(**inputs):
    inputs = {k: np.asarray(v) for k, v in inputs.items()}
    nc = _get_nc()
    in_maps = make_in_maps(inputs)
    res = run_bass_kernel_spmd(nc, in_maps, core_ids=list(range(NCORES)))
    out = np.stack([np.asarray(r["out"]).reshape(T, D) for r in res.results], axis=0)
    return out.astype(np.float32)
```

```python
from contextlib import ExitStack

import numpy as np
import concourse.bass as bass
import concourse.mybir as mybir
from concourse.bass_utils import run_bass_kernel_spmd

F32 = mybir.dt.float32
BF16 = mybir.dt.bfloat16
I32 = mybir.dt.int32
AF = mybir.ActivationFunctionType
ALU = mybir.AluOpType

NCORES = 8
D = 1024
T = 2048
NTT = 16
NTG = 4
HD = 64
NQH = 8
ATTN_W = 512
KV_W = 128
CC = 512
CW = 31
Q_OFF = 0
K_OFF = 512
V_OFF = 640
GLU_OFF = 768
GATE_OFF = 1792
IN_W = 3840
DFF = 2816
NFF = 22
EPS = 1e-5

ENGS = ("pe", "act", "dve", "pool", "sp")
NDMA_SEM = 16


class Op:
    __slots__ = ("eng", "fn", "dma", "deps", "signal", "ticket", "dsem",
                 "dticket", "idx", "gidx")

    def __init__(self, eng, fn, dma):
        self.eng = eng
        self.fn = fn
        self.dma = dma
        self.deps = []
        self.signal = False
        self.ticket = 0
        self.dsem = 0
        self.dticket = 0


class Sched:
    def __init__(self):
        self.ops = {e: [] for e in ENGS}
        self.lastw = {}
        self.readers = {}
        self.dma_hist = {e: [] for e in ENGS}
        self.n = 0

    @staticmethod
    def _rkey(op):
        return ("d", op.gidx) if op.dma else ("c", op.eng)

    def add(self, eng, fn, r=(), w=(), dma=False):
        op = Op(eng, fn, dma)
        op.gidx = self.n
        self.n += 1
        op.idx = len(self.ops[eng])
        deps = {}
        for x in r:
            d = self.lastw.get(x)
            if d is not None:
                deps[d] = True
        for x in w:
            d = self.lastw.get(x)
            if d is not None:
                deps.setdefault(d, False)
            for rd in self.readers.get(x, {}).values():
                deps.setdefault(rd, False)
        for d, raw in deps.items():
            if d is op:
                continue
            if (not dma) and (not d.dma) and d.eng == eng:
                if eng == "pe":
                    continue
            op.deps.append(d)
            d.signal = True
        if dma:
            hist = self.dma_hist[eng]
            k = len(hist)
            op.dsem = k % NDMA_SEM
            op.dticket = 16 * (k // NDMA_SEM + 1)
            if k >= NDMA_SEM:
                op.deps.append(hist[k - NDMA_SEM])
            hist.append(op)
        for x in r:
            self.readers.setdefault(x, {})[self._rkey(op)] = op
        for x in w:
            self.lastw[x] = op
            self.readers[x] = {}
        self.ops[eng].append(op)
        return op

    def alias(self, news, olds):
        if isinstance(news, str):
            news = [news]
        for new in news:
            self._alias1(new, olds)

    def _alias1(self, new, olds):
        rd = self.readers.setdefault(new, {})
        for o in olds:
            cands = list(self.readers.get(o, {}).values())
            lw = self.lastw.get(o)
            if lw is not None:
                cands.append(lw)
            for c in cands:
                k = self._rkey(c)
                if k not in rd or rd[k].gidx < c.gidx:
                    rd[k] = c

    def emit(self, nc, stack):
        for e in ENGS:
            t = 0
            for op in self.ops[e]:
                if op.dma:
                    continue
                if op.signal:
                    t += 1
                    op.ticket = t
        csem = {e: stack.enter_context(nc.semaphore("c_" + e)) for e in ENGS}
        dsem = {e: [stack.enter_context(nc.semaphore("d_%s_%d" % (e, i)))
                    for i in range(NDMA_SEM)] for e in ("sp", "pool", "act")}
        block = stack.enter_context(nc.Block())

        def run(ename):
            def body(eng):
                known = {}
                for op in self.ops[ename]:
                    waits = {}
                    for d in op.deps:
                        if d.dma:
                            key, val = ("d", d.eng, d.dsem), d.dticket
                        else:
                            key, val = ("c", d.eng), d.ticket
                        if known.get(key, 0) >= val:
                            continue
                        if waits.get(key, 0) < val:
                            waits[key] = val
                    for key, val in waits.items():
                        sem = csem[key[1]] if key[0] == "c" else dsem[key[1]][key[2]]
                        eng.wait_ge(sem, val)
                        known[key] = val
                    ins = op.fn(eng)
                    if ins is None:
                        continue
                    if op.dma:
                        ins.then_inc(dsem[ename][op.dsem], 16)
                    elif op.signal:
                        ins.then_inc(csem[ename], 1)
            return body

        block.tensor(run("pe"))
        block.scalar(run("act"))
        block.vector(run("dve"))
        block.gpsimd(run("pool"))
        block.sync(run("sp"))


class Arena:
    def __init__(self, t, nbytes):
        self.t = t
        self.nbytes = nbytes

    def f32(self, off, n):
        assert off % 4 == 0 and off + 4 * n <= self.nbytes, (off, n)
        return self.t[:, off // 4: off // 4 + n]

    def bf16(self, off, n):
        assert off % 4 == 0 and n % 2 == 0 and off + 2 * n <= self.nbytes, (off, n)
        return self.t[:, off // 4: off // 4 + n // 2].bitcast(BF16)

    def i32(self, off, n):
        return self.t[:, off // 4: off // 4 + n].bitcast(I32)


def build_program(stop_after="all", dumps=()):
    nc = bass.Bass("TRN2", target_bir_lowering=False)
    S = Sched()
    stack = ExitStack()

    def din(name, shape):
        return nc.dram_tensor(name, list(shape), F32, kind="ExternalInput").ap()

    x_d = din("x", [T, D])
    g_mix_d = din("g_mix_norm", [1, D])
    w_in_d = din("w_in", [D, IN_W])
    b_in_d = din("b_in", [1, IN_W])
    sinks_d = din("sinks", [1, NQH])
    conv_w_d = din("conv_w", [CW, CC])
    conv_b_d = din("conv_b", [1, CC])
    ln_g_d = din("ln_g", [1, CC])
    ln_b_d = din("ln_b", [1, CC])
    w_ap_d = din("w_attn_proj", [ATTN_W, D])
    w_cp_d = din("w_conv_proj", [CC, D])
    b_cp_d = din("b_conv_proj", [1, D])
    w_out_d = din("w_out", [D, D])
    g_ffn_d = din("g_ffn_norm", [1, D])
    w_fi_d = din("w_ffn_in", [D, 2 * DFF])
    w_fd_d = din("w_ffn_down", [DFF, D])
    g_fin_d = din("g_final", [1, D])
    out_d = nc.dram_tensor("out", [T, D], F32, kind="ExternalOutput").ap()

    ARENA_BYTES = 211968
    arena_t = stack.enter_context(nc.sbuf_tensor("arena", [128, ARENA_BYTES // 4], F32))
    A = Arena(arena_t, ARENA_BYTES)
    off = 0

    def region(nbytes):
        nonlocal off
        o = off
        off += (nbytes + 31) // 32 * 32
        assert off <= ARENA_BYTES, off
        return o

    X_OFF = region(65536)
    GBC_OFF = region(4096)
    IDENT_OFF = region(256)
    MASKC_OFF = region(256)
    MASKP_OFF = region(256)
    ONES_OFF = region(256)
    MB_OFF = region(2048)
    BIN_OFF = region(30 * 4)
    BQP_OFF = region(4 * 4)
    BVBC_OFF = region(128 * 4)
    CONVW_OFF = region(4 * CW * 4)
    CVEC_OFF = region(3 * 4 * 4)
    BCP_OFF = region(8 * 4)
    SINK_OFF = region(8 * 4)
    SS_OFF = region(48 * 4)
    RSTD_OFF = region(48 * 4)
    IOTA_OFF = region(128 * 4)
    EPSB_OFF = region(32)
    EM_OFF = region(64)
    WCOL_OFF = region(16 * 8 * 4)
    H_OFF = region(32768)
    M_OFF = region(41280)
    WOUT_OFF = region(16384)
    TMP_OFF = region(28736)
    NRING = 8
    RING_OFF = region(2048 * NRING)

    x_tm = A.f32(X_OFF, 16 * 1024).rearrange("p (n d) -> p n d", n=16)
    attnT = A.bf16(X_OFF, 4 * T).rearrange("p (c t) -> p c t", c=4)
    convT = A.bf16(X_OFF + 16384, 4 * T).rearrange("p (c t) -> p c t", c=4)
    wS = A.bf16(X_OFF + 32768, 16 * 8 * 32).rearrange("p (g m c) -> p g m c", g=16, m=8)
    ZSW = 544
    zs = [A.bf16(X_OFF + 40960 + 8704 * i, 2 * 4 * ZSW).rearrange("p (h g t) -> p h g t", h=2, g=4)
          for i in range(2)]
    xstage = A.f32(X_OFF + 49152, 4 * 1024).rearrange("p (s d) -> p s d", s=4)

    gbc = A.f32(GBC_OFF, 1024)
    ident = A.bf16(IDENT_OFF, 128)
    mask_c = A.bf16(MASKC_OFF, 128)
    mask_p = A.bf16(MASKP_OFF, 128)
    ones_bf = A.bf16(ONES_OFF, 128)
    maskb = A.bf16(MB_OFF, 1024).rearrange("p (w h q) -> p w h q", w=2, h=4)
    b_in_fm = A.f32(BIN_OFF, 30)
    bq_perm = A.f32(BQP_OFF, 4)
    bv_bc = A.f32(BVBC_OFF, 128)
    convw = A.f32(CONVW_OFF, 4 * CW).rearrange("p (c j) -> p c j", c=4)
    cvec = A.f32(CVEC_OFF, 12).rearrange("p (v c) -> p v c", v=3)
    bcp = A.f32(BCP_OFF, 8)
    expsink = A.f32(SINK_OFF, 8)
    ss = A.f32(SS_OFF, 48)
    rstd = A.f32(RSTD_OFF, 48)
    iota_i = A.i32(IOTA_OFF, 128)
    epsb = A.f32(EPSB_OFF, 1)
    emask = A.bf16(EM_OFF, 32)
    wcol = A.f32(WCOL_OFF, 128).rearrange("p (g m) -> p g m", g=16)
    hT = A.bf16(H_OFF, 8 * T).rearrange("p (c t) -> p c t", c=8)
    qT = A.bf16(M_OFF, 4 * T).rearrange("p (c t) -> p c t", c=4)
    kT = A.bf16(M_OFF + 16384, T)
    v_aug = A.bf16(M_OFF + 20480, 16 * 2 * 65).rearrange("p (n g d) -> p n g d", n=16, g=2)
    ZPAD = 32
    zT = A.bf16(M_OFF + 24640, 4 * (T + ZPAD)).rearrange("p (c t) -> p c t", c=4)
    mergedT = A.bf16(M_OFF, 8 * T).rearrange("p (c t) -> p c t", c=8)
    w_out_sb = A.bf16(WOUT_OFF, 8 * 1024).rearrange("p (k n) -> p k n", k=8)
    ring = [A.bf16(RING_OFF + 2048 * i, 1024).rearrange("p (k m) -> p k m", k=8)
            for i in range(NRING)]
    NXN = 3
    xn = A.bf16(TMP_OFF, NXN * 1024).rearrange("p (b d) -> p b d", b=NXN)
    NPT = 3
    ptb = A.bf16(TMP_OFF, NPT * 2 * 512).rearrange("p (s w q) -> p s w q", s=NPT, w=2)
    attn_tm = A.bf16(TMP_OFF + 6144, 2 * 512).rearrange("p (b f) -> p b f", b=2)
    den = A.f32(TMP_OFF + 8192, 2 * 8).rearrange("p (b h) -> p b h", b=2)
    y32 = A.f32(TMP_OFF + 8256, 4 * 512).rearrange("p (c t) -> p c t", c=4)
    sig = A.f32(TMP_OFF + 8256, 2 * 512).rearrange("p (b t) -> p b t", b=2)
    ybf = A.bf16(TMP_OFF + 16448, 4 * 512).rearrange("p (c t) -> p c t", c=4)
    ysq = A.bf16(TMP_OFF + 20544, 4 * 512).rearrange("p (c t) -> p c t", c=4)
    w32 = arena_t[0:32, (TMP_OFF + 20544) // 4:(TMP_OFF + 20544) // 4 + 512]
    w32b = arena_t[0:32, (TMP_OFF + 16448) // 4:(TMP_OFF + 16448) // 4 + 256].bitcast(BF16)
    stA = A.f32(TMP_OFF + 24640, 512)
    stB = A.f32(TMP_OFF + 26688, 512)
    etmp = A.f32(TMP_OFF + 8256, 8 * 512).rearrange("p (b k t) -> p b k t", b=2, k=4)
    FF_GROUPS = [6, 6, 5, 5]
    actT = A.bf16(M_OFF, 6 * T).rearrange("p (f t) -> p f t", f=6)
    wd_sb = [A.bf16(M_OFF + 24576 + 12288 * i, 6 * 1024).rearrange("p (f n) -> p f n", f=6)
             for i in range(2)]
    sgt = A.f32(M_OFF + 49152, 2 * 512).rearrange("p (b t) -> p b t", b=2)

    ps = [stack.enter_context(nc.psum_tensor("ps%d" % i, [128, 512], F32)) for i in range(8)]
    ps_bf = [p.bitcast(BF16) for p in ps]
    bank_ctr = [0]

    def next_bank():
        b = bank_ctr[0] % 8
        bank_ctr[0] += 1
        return b

    def dma(eng, out, in_, r=(), w=(), **kw):
        return S.add(eng, lambda e: e.dma_start(out=out, in_=in_, **kw), r=r, w=w, dma=True)

    def op(eng, fn, r=(), w=()):
        return S.add(eng, fn, r=r, w=w)

    def mm(out, lhsT, rhs, start, stop, r, w):
        return op("pe", lambda e: e.matmul(out=out, lhsT=lhsT, rhs=rhs, start=start, stop=stop), r, w)

    def act(out, in_, func, r, w, **kw):
        return op("act", lambda e: e.activation(out=out, in_=in_, func=func, **kw), r, w)

    def tt_(eng, out, in0, in1, alu, r, w):
        return op(eng, lambda e: e.tensor_tensor(out=out, in0=in0, in1=in1, op=alu), r, w)

    def ts_(eng, out, in0, s1, s2, op0, op1, r, w):
        if s2 is None:
            return op(eng, lambda e: e.tensor_scalar(out=out, in0=in0, scalar1=s1, scalar2=None,
                                                     op0=op0), r, w)
        return op(eng, lambda e: e.tensor_scalar(out=out, in0=in0, scalar1=s1, scalar2=s2,
                                                 op0=op0, op1=op1), r, w)

    def stt_(eng, out, in0, scalar, in1, op0, op1, r, w):
        return op(eng, lambda e: e.scalar_tensor_tensor(out=out, in0=in0, scalar=scalar, in1=in1,
                                                        op0=op0, op1=op1), r, w)

    def hres(tg):
        return ["hT.%d" % t for t in range(4 * tg, 4 * tg + 4)]

    dump_list = []
    dumps_avail = {}

    def finish():
        for nm in dumps:
            ap, res = dumps_avail[nm]
            d = nc.dram_tensor("dbg_" + nm, list(ap.shape), ap.dtype, kind="ExternalOutput").ap()
            dma("sp", d, ap, r=res, w=["dbg_" + nm])
            dump_list.append("dbg_" + nm)
        S.add("sp", lambda e: None, r=["out.%d" % t for t in range(NTT)] + dump_list)
        S.emit(nc, stack)
        stack.close()
        return nc

    wtasks = []
    wstate = {"issued": 0, "consumed": 0, "done": 0}

    def w_issue_upto(n):
        while wstate["issued"] < min(n, len(wtasks)):
            i = wstate["issued"]
            wtasks[i](i % NRING)
            wstate["issued"] += 1

    def w_next():
        i = wstate["consumed"]
        wstate["consumed"] += 1
        assert i < wstate["issued"], (i, wstate)
        return i % NRING

    def w_done(n=1):
        wstate["done"] += n
        w_issue_upto(wstate["done"] + NRING)

    def wtask_cols(src, col_blocks):
        def t(r):
            for i_, (m0, c0, wd) in enumerate(col_blocks):
                wl = ["ring%d.%d" % (r, i_)] + (["ring%d.1" % r] if len(col_blocks) == 1 else [])
                dma("pool", ring[r][:, :, m0:m0 + wd],
                    src[:, c0:c0 + wd].rearrange("(k p) m -> p k m", p=128),
                    w=wl)
        return t

    def wtask_proj(c):
        def t(r):
            dma("pool", ring[r][:, 0:4, :],
                w_ap_d[:, c * 128:(c + 1) * 128].rearrange("(k p) m -> p k m", p=128),
                w=["ring%d.0" % r])
            dma("pool", ring[r][:, 4:8, :],
                w_cp_d[:, c * 128:(c + 1) * 128].rearrange("(k p) m -> p k m", p=128),
                w=["ring%d.1" % r])
        return t

    for c in range(4):
        wtasks.append(wtask_cols(w_in_d, [(0, Q_OFF + 64 * c, 64), (64, Q_OFF + 64 * (c + 4), 64)]))
    wtasks.append(wtask_cols(w_in_d, [(0, K_OFF, 128)]))
    wtasks.append(wtask_cols(w_in_d, [(0, V_OFF, 128)]))
    for c in range(4):
        wtasks.append(wtask_cols(w_in_d, [(0, GLU_OFF + 128 * c, 128)]))
        wtasks.append(wtask_cols(w_in_d, [(0, GLU_OFF + CC + 128 * c, 128)]))
    for c in range(8):
        wtasks.append(wtask_proj(c))
        wtasks.append(wtask_cols(w_in_d, [(0, GATE_OFF + 128 * c, 128)]))
        wtasks.append(wtask_cols(w_in_d, [(0, GATE_OFF + D + 128 * c, 128)]))
    for f in range(NFF):
        wtasks.append(wtask_cols(w_fi_d, [(0, 128 * f, 128)]))
        wtasks.append(wtask_cols(w_fi_d, [(0, DFF + 128 * f, 128)]))

    NCD = dict(allow_slow_non_contiguous=True)
    xv = x_d.rearrange("(n p) d -> p n d", p=128)
    dma("sp", gbc, g_mix_d.broadcast_to([128, D]), w=["gbc"])
    for tt in range(4):
        dma("sp", xstage[:, tt, :], xv[:, tt, :], w=["xs%d" % tt])
    dma("sp", b_in_fm, b_in_d.rearrange("o (c p) -> p (o c)", p=128), w=["b_in_fm"], **NCD)
    for g in range(2):
        dma("sp", bq_perm[64 * g:64 * (g + 1), :],
            b_in_d[:, 256 * g:256 * (g + 1)].rearrange("o (c p) -> p (o c)", p=64),
            w=["bq_perm%d" % g], **NCD)
    dma("sp", bv_bc, b_in_d[:, V_OFF:V_OFF + 128].broadcast_to([128, 128]), w=["bv_bc"])
    dma("sp", expsink, sinks_d.broadcast_to([128, NQH]), w=["expsink"])

    op("pool", lambda e: e.memset(ss, 0.0), w=["ss"])
    op("pool", lambda e: e.memset(epsb, EPS), w=["epsb"])
    op("pool", lambda e: e.iota(iota_i, [[1, 128]], base=0, channel_multiplier=-1), w=["iota"])
    ts_("pool", ident, iota_i, 0, None, ALU.is_equal, None, r=["iota"], w=["ident"])
    ts_("pool", mask_c, iota_i, 0, None, ALU.is_ge, None, r=["iota"], w=["mask_c"])
    ts_("pool", mask_p, iota_i, 0, None, ALU.is_lt, None, r=["iota"], w=["mask_p"])
    op("pool", lambda e: e.memset(ones_bf, 1.0 / CC), w=["ones"])
    op("pool", lambda e: e.memset(w32[0:1, :], 0.0), w=["w32z"])

    w_issue_upto(NRING)

    def late_consts():
        dma("sp", w32[1:32, :], conv_w_d, r=[], w=["w32"])
        dma("sp", cvec[:, 0, :], conv_b_d.rearrange("o (c p) -> p (o c)", p=128), w=["cvec0"], **NCD)
        dma("sp", cvec[:, 1, :], ln_g_d.rearrange("o (c p) -> p (o c)", p=128), w=["cvec1"], **NCD)
        dma("sp", cvec[:, 2, :], ln_b_d.rearrange("o (c p) -> p (o c)", p=128), w=["cvec2"], **NCD)
        dma("sp", bcp, b_cp_d.rearrange("o (c p) -> p (o c)", p=128), w=["bcp"], **NCD)

    def rms_stage1a(src_tile, tt, si, src_res):
        b = tt % NXN
        col = 16 * si + tt
        sres = "ss%d" % col
        rres = "rstd%d" % col
        act(xn[:, b, :], src_tile, AF.Square, r=[src_res, "ss"], w=["xn%d" % b, sres],
            accum_out=ss[:, col:col + 1])
        act(rstd[:, col:col + 1], ss[:, col:col + 1], AF.Sqrt, r=[sres, "epsb"], w=[rres],
            scale=1.0 / D, bias=epsb)

    def rms_stage1b(src_tile, tt, si, src_res):
        b = tt % NXN
        col = 16 * si + tt
        rres = "rstd%d" % col
        op("dve", lambda e: e.reciprocal(out=rstd[:, col:col + 1], in_=rstd[:, col:col + 1]),
           r=[rres], w=[rres])
        stt_("dve", xn[:, b, :], src_tile, rstd[:, col:col + 1], gbc, ALU.mult, ALU.mult,
             r=[src_res, rres, "gbc"], w=["xn%d" % b])

    def rms_stage1(src_tile, tt, si, src_res):
        rms_stage1a(src_tile, tt, si, src_res)
        rms_stage1b(src_tile, tt, si, src_res)

    def rms_stage2(tt, dstT, dst_pref, copy_eng="act"):
        b = tt % NXN
        bank = next_bank()
        for c in range(8):
            op("pe", lambda e, c=c: e.transpose(out=ps_bf[bank][:, c * 128:(c + 1) * 128],
                                                in_=xn[:, b, c * 128:(c + 1) * 128],
                                                identity=ident),
               r=["xn%d" % b, "ident"], w=["ps%d" % bank])
        if copy_eng == "act":
            act(dstT[:, :, tt * 128:(tt + 1) * 128],
                ps_bf[bank].rearrange("p (c j) -> p c j", c=8), AF.Copy,
                r=["ps%d" % bank], w=["%s.%d" % (dst_pref, tt)])
        else:
            op("dve", lambda e: e.tensor_copy(out=dstT[:, :, tt * 128:(tt + 1) * 128],
                                              in_=ps_bf[bank].rearrange("p (c j) -> p c j", c=8)),
               r=["ps%d" % bank], w=["%s.%d" % (dst_pref, tt)])

    SCALE = HD ** -0.5

    def fm_matmuls(r, tg, bank):
        for k in range(8):
            mm(ps[bank][:, :], ring[r][:, k, :], hT[:, k, tg * 512:(tg + 1) * 512],
               k == 0, k == 7, r=["ring%d.0" % r, "ring%d.1" % r] + hres(tg), w=["ps%d" % bank])

    op("pool", lambda e: e.memset(v_aug[:, :, :, 64:65], 1.0), w=["v_ones"])
    op("pool", lambda e: e.memset(zT[:, :, 0:ZPAD], 0.0), w=["z_pad"])
    rq = [w_next() for _ in range(4)]
    rk = w_next()
    rv = w_next()

    def part1(tg):
        for c in range(4):
            bank = next_bank()
            fm_matmuls(rq[c], tg, bank)
            act(qT[:, c, tg * 512:(tg + 1) * 512], ps[bank][:, :], AF.Identity,
                r=["ps%d" % bank, "bq_perm0", "bq_perm1"], w=["qT.%d" % tg], bias=bq_perm[:, c:c + 1])
        bank = next_bank()
        fm_matmuls(rk, tg, bank)
        act(kT[:, tg * 512:(tg + 1) * 512], ps[bank][:, :], AF.Identity,
            r=["ps%d" % bank, "b_in_fm"], w=["kT.%d" % tg], bias=b_in_fm[:, 4:5])
        bank = next_bank()
        for j in range(4):
            tt = 4 * tg + j
            for k in range(8):
                mm(ps[bank][:, j * 128:(j + 1) * 128], hT[:, k, tt * 128:(tt + 1) * 128],
                   ring[rv][:, k, :], k == 0, k == 7,
                   r=["ring%d.0" % rv, "ring%d.1" % rv, "hT.%d" % tt], w=["ps%d" % bank])
        tt_("dve", v_aug[:, 4 * tg:4 * tg + 4, :, 0:64],
            ps[bank].rearrange("p (n g d) -> p n g d", n=4, g=2),
            bv_bc.rearrange("p (g d) -> p g d", g=2).unsqueeze(1).broadcast_to([128, 4, 2, 64]),
            ALU.add, r=["ps%d" % bank, "bv_bc"], w=["v.%d" % tg])

    for tt in range(NTT):
        st = tt % 4
        if tt >= 4:
            dma("sp", xstage[:, st, :], xv[:, tt, :], w=["xs%d" % st])
        if tt == 15:
            late_consts()
        rms_stage1(xstage[:, st, :], tt, 0, "xs%d" % st)
        if tt >= 1:
            rms_stage2(tt - 1, hT, "hT", copy_eng="dve")
        if tt >= 4 and tt % 4 == 1:
            part1(tt // 4 - 1)
    rms_stage2(NTT - 1, hT, "hT", copy_eng="dve")
    part1(3)
    w_done(6)
    act(expsink, expsink, AF.Exp, r=["expsink"], w=["expsink"])
    NEGB = 30000.0
    ts_("dve", maskb[:, 1, :, :], mask_c.unsqueeze(1).broadcast_to([128, 4, 128]), -1.0, NEGB,
        ALU.add, ALU.mult, r=["mask_c"], w=["maskb1"])
    ts_("dve", maskb[:, 0, :, :], mask_p.unsqueeze(1).broadcast_to([128, 4, 128]), -1.0, NEGB,
        ALU.add, ALU.mult, r=["mask_p"], w=["maskb0"])

    op("dve", lambda e: e.tensor_copy(out=w32b, in_=w32), r=["w32", "w32z"], w=["w32b"])
    wbank = next_bank()
    selv = ident[0:32, 0:32].rearrange("p (m q) -> p m q", q=4)
    for G in range(16):
        for jj in range(4):
            op("pe", lambda e, G=G, jj=jj: e.matmul(
                out=ps[wbank][32 * jj:32 * (jj + 1), 8 * G:8 * (G + 1)],
                lhsT=w32b[:, 32 * G:32 * (G + 1)], rhs=selv[:, :, jj],
                start=True, stop=True, tile_position=(0, 32 * jj)),
               r=["w32b", "ident"], w=["ps%d" % wbank])
    op("dve", lambda e: e.tensor_copy(out=wcol, in_=ps[wbank][:, 0:128].rearrange("p (g m) -> p g m", g=16)),
       r=["ps%d" % wbank], w=["wcol"])

    dumps_avail["hT"] = (hT, ["hT.%d" % t for t in range(NTT)])
    if stop_after == "A":
        return finish()

    S.alias(["wS"] + ["zs%d.%d.%d" % (h_, g_, j_) for h_ in range(2) for g_ in range(4) for j_ in range(4)],
            ["xs%d" % i for i in range(4)])
    tt_("dve", emask, ident[:, 0:32], ident[:, 32:64], ALU.add, r=["ident"], w=["emask"])
    tt_("dve", emask, emask, ident[:, 64:96], ALU.add, r=["ident", "emask"], w=["emask"])
    tt_("dve", emask, emask, ident[:, 96:128], ALU.add, r=["ident", "emask"], w=["emask"])

    def build_diag(c):
        for G in range(4 * c, 4 * c + 4):
            tt_("dve", wS[:, G, :, :], emask.unsqueeze(1).broadcast_to([128, 8, 32]),
                wcol[:, G, :].unsqueeze(2).broadcast_to([128, 8, 32]), ALU.mult,
                r=["emask", "wcol"], w=["wS"])

    def fill_zs(tg, half):
        t0 = tg * 512 + (ZPAD - 31)
        for cg in range(4):
            for jj in range(4):
                zr = ["z_pad"] + ["zT.%d.%d" % (c_, t_) for c_ in (2 * half, 2 * half + 1)
                                  for t_ in ((tg - 1, tg) if tg > 0 else (tg,))]
                dma("sp" if (cg < 2 or tg == 0) else "pool", zs[half][32 * jj:32 * (jj + 1), :, cg, 0:540],
                    zT[32 * cg:32 * (cg + 1), 2 * half:2 * half + 2, t0 + jj:t0 + jj + 540],
                    r=zr, w=["zs%d.%d.%d" % (half, cg, jj)])

    for c in range(4):
        ra = w_next()
        rb = w_next()
        for tg in range(NTG):
            ba = next_bank()
            fm_matmuls(ra, tg, ba)
            bb = next_bank()
            fm_matmuls(rb, tg, bb)
            sb = (c * NTG + tg) % 2
            act(sig[:, sb, :], ps[bb][:, :], AF.Sigmoid, r=["ps%d" % bb, "b_in_fm"], w=["sig%d" % sb],
                bias=b_in_fm[:, 10 + c:11 + c])
            stt_("dve", zT[:, c, ZPAD + tg * 512:ZPAD + (tg + 1) * 512], ps[ba][:, :],
                 b_in_fm[:, 6 + c:7 + c], sig[:, sb, :], ALU.add, ALU.mult,
                 r=["ps%d" % ba, "sig%d" % sb, "b_in_fm"], w=["zT.%d.%d" % (c, tg)])
        w_done(2)
        build_diag(c)

    dumps_avail["qT"] = (qT, ["qT.%d" % t for t in range(NTG)])
    dumps_avail["kT"] = (kT, ["kT.%d" % t for t in range(NTG)])
    dumps_avail["v_aug"] = (v_aug, ["v.%d" % t for t in range(NTG)] + ["v_ones"])
    dumps_avail["zT"] = (zT, ["zT.%d.%d" % (c, t) for c in range(4) for t in range(NTG)] + ["z_pad"])
    if stop_after == "B":
        return finish()

    for hh in range(2):
        dma("pool", w_out_sb[:, :, hh * 512:(hh + 1) * 512],
            w_out_d[:, hh * 512:(hh + 1) * 512].rearrange("(k p) n -> p k n", p=128),
            w=["w_out%d" % hh])

    S.alias(["pt.%d.%d" % (s_, w_) for s_ in range(NPT) for w_ in range(2)], ["xn0", "xn1", "xn2"])
    S.alias(["y32.%d" % c_ for c_ in range(4)], ["sig0", "sig1"])
    S.alias(["ybf.%d" % c_ for c_ in range(4)] + ["ysq.%d" % c_ for c_ in range(4)], ["w32", "w32z", "w32b"])
    steps = [(n, g) for n in range(NTT) for g in range(2)]

    def emit_scores(si):
        n, g = steps[si]
        s = si % NPT
        rows = slice(64 * g, 64 * g + 64)
        q_rhs = qT[rows, :, n * 128:(n + 1) * 128]
        qres = ["qT.%d" % (n // 4)]
        blocks = [(1, n)] + ([(0, n - 1)] if n > 0 else [])
        for (wsel, kb) in blocks:
            bank = next_bank()
            mm(ps[bank][:, :], kT[rows, kb * 128:(kb + 1) * 128], q_rhs, True, False,
               r=qres + ["kT.%d" % (kb // 4)], w=["ps%d" % bank])
            mm(ps[bank][:, :], ident, maskb[:, wsel, :, :].rearrange("p h q -> p (h q)"), False, True,
               r=["ident", "maskb%d" % wsel], w=["ps%d" % bank])
            pres = "pt.%d.%d" % (s, wsel)
            act(ptb[:, s, wsel, :], ps[bank][:, :], AF.Exp, r=["ps%d" % bank], w=[pres],
                scale=SCALE)

    def emit_pv(si):
        n, g = steps[si]
        s = si % NPT
        b = n % 2
        bank = next_bank()
        for i in range(4):
            o = ps[bank][:, i * 65:(i + 1) * 65]
            if n > 0:
                mm(o, ptb[:, s, 0, i * 128:(i + 1) * 128], v_aug[:, n - 1, g, :], True, False,
                   r=["pt.%d.0" % s, "v.%d" % ((n - 1) // 4), "v_ones"], w=["ps%d" % bank])
            mm(o, ptb[:, s, 1, i * 128:(i + 1) * 128], v_aug[:, n, g, :], n == 0, True,
               r=["pt.%d.1" % s, "v.%d" % (n // 4), "v_ones"], w=["ps%d" % bank])
        pso = ps[bank][:, 0:260].rearrange("p (h d) -> p h d", h=4)
        dres = "den%d.%d" % (b, g)
        tt_("dve", den[:, b, 4 * g:4 * g + 4], pso[:, :, 64], expsink[:, 4 * g:4 * g + 4], ALU.add,
            r=["ps%d" % bank, "expsink"], w=[dres])
        op("dve", lambda e: e.reciprocal(out=den[:, b, 4 * g:4 * g + 4], in_=den[:, b, 4 * g:4 * g + 4]),
           r=[dres], w=[dres])
        tt_("dve", attn_tm[:, b, 256 * g:256 * (g + 1)].rearrange("p (h d) -> p h d", h=4),
            pso[:, :, 0:64],
            den[:, b, 4 * g:4 * g + 4].unsqueeze(2).broadcast_to([128, 4, 64]), ALU.mult,
            r=["ps%d" % bank, dres], w=["attn_tm%d.%d" % (b, g)])

    def emit_attn_T(n):
        b = n % 2
        bank = next_bank()
        for c in range(4):
            op("pe", lambda e, c=c: e.transpose(out=ps_bf[bank][:, c * 128:(c + 1) * 128],
                                                in_=attn_tm[:, b, c * 128:(c + 1) * 128],
                                                identity=ident),
               r=["attn_tm%d.0" % b, "attn_tm%d.1" % b, "ident"], w=["ps%d" % bank])
        act(attnT[:, :, n * 128:(n + 1) * 128],
            ps_bf[bank][:, 0:512].rearrange("p (c j) -> p c j", c=4), AF.Copy,
            r=["ps%d" % bank], w=["attnT.%d" % n])

    conv_units = [(tg, c) for tg in range(NTG) for c in range(4)]

    def emit_conv_unit(ui):
        tg, c = conv_units[ui]
        bank = next_bank()
        half = c // 2
        for m in range(8):
            for cg in range(4):
                op("pe", lambda e, m=m, cg=cg: e.matmul(
                    out=ps[bank][32 * cg:32 * (cg + 1), :], lhsT=wS[:, 4 * c + cg, m, :],
                    rhs=zs[half][:, c % 2, cg, 4 * m:4 * m + 512],
                    start=(m == 0), stop=(m == 7), tile_position=(0, 32 * cg)),
                   r=["wS"] + ["zs%d.%d.%d" % (half, cg, j_) for j_ in range(4)], w=["ps%d" % bank])
        act(y32[:, c, :], ps[bank][:, :], AF.Identity, r=["ps%d" % bank, "cvec0", "cvec1", "cvec2"],
            w=["y32.%d" % c], bias=cvec[:, 0, c:c + 1])
        act(ysq[:, c, :], ps[bank][:, :], AF.Square, r=["ps%d" % bank, "cvec0", "cvec1", "cvec2"],
            w=["ysq.%d" % c], bias=cvec[:, 0, c:c + 1])
        op("dve", lambda e: e.tensor_copy(out=ybf[:, c, :], in_=y32[:, c, :]),
           r=["y32.%d" % c], w=["ybf.%d" % c])

    def ln_a(tg):
        bm = next_bank()
        for c in range(4):
            mm(ps[bm][:, :], ones_bf, ybf[:, c, :], c == 0, c == 3,
               r=["ones", "ybf.%d" % c], w=["ps%d" % bm])
        bq = next_bank()
        for c in range(4):
            mm(ps[bq][:, :], ones_bf, ysq[:, c, :], c == 0, c == 3,
               r=["ones", "ysq.%d" % c], w=["ps%d" % bq])
        act(stA, ps[bm][:, :], AF.Square, r=["ps%d" % bm], w=["stA"])
        tt_("dve", stB, ps[bq][:, :], stA, ALU.subtract, r=["ps%d" % bq, "stA"], w=["stB"])
        act(stB, stB, AF.Ln, r=["stB", "epsb"], w=["stB"], bias=epsb)
        act(stB, stB, AF.Exp, r=["stB"], w=["stB"], scale=-0.5)
        tt_("dve", stA, ps[bm][:, :], stB, ALU.mult, r=["ps%d" % bm, "stB"], w=["stA"])

    def ln_b(tg, c):
        tt_("dve", y32[:, c, :], y32[:, c, :], stB, ALU.mult, r=["y32.%d" % c, "stB"],
            w=["y32.%d" % c])
        tt_("dve", y32[:, c, :], y32[:, c, :], stA, ALU.subtract, r=["y32.%d" % c, "stA"],
            w=["y32.%d" % c])
        act(convT[:, c, tg * 512:(tg + 1) * 512], y32[:, c, :], AF.Silu,
            r=["y32.%d" % c, "cvec0", "cvec1", "cvec2"], w=["convT.%d.%d" % (c, tg)],
            scale=cvec[:, 1, c:c + 1], bias=cvec[:, 2, c:c + 1])

    fill_zs(0, 0)
    fill_zs(0, 1)
    LOOK = NPT - 1
    for si in range(min(LOOK, len(steps))):
        emit_scores(si)
    cu = 0
    for si in range(len(steps)):
        if si + LOOK < len(steps):
            emit_scores(si + LOOK)
        emit_pv(si)
        if steps[si][1] == 1:
            n = steps[si][0]
            if n >= 4 and n % 4 == 0:
                for c_ in range(4):
                    ln_b(n // 4 - 1, c_)
            emit_conv_unit(cu)
            cu += 1
            if n % 4 == 1 and n // 4 + 1 < NTG:
                fill_zs(n // 4 + 1, 0)
            if n % 4 == 3 and n // 4 + 1 < NTG:
                fill_zs(n // 4 + 1, 1)
            if n % 4 == 3:
                ln_a(n // 4)
            if n >= 1:
                emit_attn_T(n - 1)
    emit_attn_T(NTT - 1)
    for c in range(4):
        ln_b(NTG - 1, c)
    assert cu == len(conv_units)

    dumps_avail["attnT"] = (attnT, ["attnT.%d" % n for n in range(NTT)])
    dumps_avail["convT"] = (convT, ["convT.%d.%d" % (c, t) for c in range(4) for t in range(NTG)])
    if stop_after == "D":
        return finish()

    S.alias(["mergedT.%d.%d" % (c_, t_) for c_ in range(8) for t_ in range(NTG)], ["qT.%d" % t for t in range(NTG)] + ["kT.%d" % t for t in range(NTG)]
            + ["v.%d" % t for t in range(NTG)] + ["v_ones", "z_pad"]
            + ["zT.%d.%d" % (c, t) for c in range(4) for t in range(NTG)])
    S.alias(["etmp.%d.%d" % (b_, i_) for b_ in range(2) for i_ in range(4)], ["y32.0", "y32.1", "y32.2", "y32.3", "ybf.0", "ybf.1", "ybf.2", "ybf.3",
                     "ysq.0", "ysq.1", "ysq.2", "ysq.3"])
    attn_res = [["attnT.%d" % n for n in range(4 * tg, 4 * tg + 4)] for tg in range(NTG)]
    ei = 0
    for c in range(8):
        rp = w_next()
        rg0 = w_next()
        rg1 = w_next()
        for tg in range(NTG):
            tsl = slice(tg * 512, (tg + 1) * 512)
            eb = ei % 2
            ei += 1
            bya = next_bank()
            for k in range(4):
                mm(ps[bya][:, :], ring[rp][:, k, :], attnT[:, k, tsl], k == 0, k == 3,
                   r=["ring%d.0" % rp, "ring%d.1" % rp] + attn_res[tg], w=["ps%d" % bya])
            byc = next_bank()
            for k in range(4):
                mm(ps[byc][:, :], ring[rp][:, 4 + k, :], convT[:, k, tsl], k == 0, k == 3,
                   r=["ring%d.0" % rp, "ring%d.1" % rp] + ["convT.%d.%d" % (k, tg)], w=["ps%d" % byc])
            bg0 = next_bank()
            fm_matmuls(rg0, tg, bg0)
            bg1 = next_bank()
            fm_matmuls(rg1, tg, bg1)
            er = ["etmp.%d.%d" % (eb, i) for i in range(4)]
            act(etmp[:, eb, 0, :], ps[bg0][:, :], AF.Sigmoid, r=["ps%d" % bg0, "b_in_fm"],
                w=[er[0]], bias=b_in_fm[:, 14 + c:15 + c])
            act(etmp[:, eb, 1, :], ps[bg1][:, :], AF.Sigmoid, r=["ps%d" % bg1, "b_in_fm"],
                w=[er[1]], bias=b_in_fm[:, 22 + c:23 + c])
            tt_("dve", etmp[:, eb, 2, :], ps[bya][:, :], etmp[:, eb, 0, :], ALU.mult,
                r=["ps%d" % bya, er[0]], w=[er[2]])
            stt_("dve", etmp[:, eb, 3, :], ps[byc][:, :], bcp[:, c:c + 1], etmp[:, eb, 1, :],
                 ALU.add, ALU.mult, r=["ps%d" % byc, er[1], "bcp"], w=[er[3]])
            tt_("dve", mergedT[:, c, tsl], etmp[:, eb, 2, :], etmp[:, eb, 3, :], ALU.add,
                r=[er[2], er[3]], w=["mergedT.%d.%d" % (c, tg)])
        w_done(3)

    dumps_avail["mergedT"] = (mergedT, ["mergedT.%d.%d" % (c, t) for c in range(8) for t in range(NTG)])
    if stop_after == "E":
        return finish()

    S.alias(["x.%d" % t_ for t_ in range(NTT)], ["attnT.%d" % n for n in range(NTT)]
            + ["convT.%d.%d" % (c, t) for c in range(4) for t in range(NTG)] + ["wS"]
            + ["zs%d.%d.%d" % (h_, g_, j_) for h_ in range(2) for g_ in range(4) for j_ in range(4)])
    S.alias("xn0", ["pt.%d.%d" % (s, w) for s in range(NPT) for w in range(2)])
    S.alias("xn1", ["pt.%d.%d" % (s, w) for s in range(NPT) for w in range(2)])
    S.alias("xn2", ["pt.%d.%d" % (s, w) for s in range(NPT) for w in range(2)])
    dma("sp", gbc, g_ffn_d.broadcast_to([128, D]), w=["gbc"])
    for tt in range(NTT):
        dma("sp", x_tm[:, tt, :], xv[:, tt, :], w=["x.%d" % tt])
    for tt in range(NTT):
        for fh in range(2):
            bank = next_bank()
            for k in range(8):
                mm(ps[bank][:, :], mergedT[:, k, tt * 128:(tt + 1) * 128],
                   w_out_sb[:, k, fh * 512:(fh + 1) * 512], k == 0, k == 7,
                   r=["mergedT.%d.%d" % (k, tt // 4), "w_out%d" % fh], w=["ps%d" % bank])
            tt_("dve", x_tm[:, tt, fh * 512:(fh + 1) * 512], x_tm[:, tt, fh * 512:(fh + 1) * 512],
                ps[bank][:, :], ALU.add, r=["x.%d" % tt, "ps%d" % bank], w=["x.%d" % tt])
        rms_stage1a(x_tm[:, tt, :], tt, 1, "x.%d" % tt)
        if tt >= 1:
            rms_stage1b(x_tm[:, tt - 1, :], tt - 1, 1, "x.%d" % (tt - 1))
        if tt >= 2:
            rms_stage2(tt - 2, hT, "hT")
    rms_stage1b(x_tm[:, NTT - 1, :], NTT - 1, 1, "x.%d" % (NTT - 1))
    rms_stage2(NTT - 2, hT, "hT")
    rms_stage2(NTT - 1, hT, "hT")

    dumps_avail["x_tm"] = (x_tm, ["x.%d" % t for t in range(NTT)])
    if stop_after == "F":
        return finish()

    ffn_old = (["w_out0", "w_out1"]
               + ["mergedT.%d.%d" % (c, t) for c in range(8) for t in range(NTG)])
    S.alias(["actT.%d.%d" % (f_, t_) for f_ in range(6) for t_ in range(NTG)]
            + ["wd0", "wd1", "sgt0", "sgt1"], ffn_old)
    ov = out_d.rearrange("(n p) d -> p n d", p=128)

    def final_norm(tt):
        col = 32 + tt
        sres = "ss%d" % col
        rres = "rstd%d" % col
        act(xn[:, 0, :], x_tm[:, tt, :], AF.Square, r=["x.%d" % tt, "ss"], w=["xn0", sres],
            accum_out=ss[:, col:col + 1])
        act(rstd[:, col:col + 1], ss[:, col:col + 1], AF.Sqrt, r=[sres, "epsb"], w=[rres],
            scale=1.0 / D, bias=epsb)
        op("dve", lambda e: e.reciprocal(out=rstd[:, col:col + 1], in_=rstd[:, col:col + 1]),
           r=[rres], w=[rres])
        stt_("dve", x_tm[:, tt, :], x_tm[:, tt, :], rstd[:, col:col + 1], gbc, ALU.mult, ALU.mult,
             r=["x.%d" % tt, rres, "gbc"], w=["x.%d" % tt])
        dma("sp", ov[:, tt, :], x_tm[:, tt, :], r=["x.%d" % tt], w=["out.%d" % tt])

    f0 = 0
    si_ = 0
    for gi, nf in enumerate(FF_GROUPS):
        last = gi == len(FF_GROUPS) - 1
        if last:
            dma("sp", gbc, g_fin_d.broadcast_to([128, D]), w=["gbc"])
        wb = gi % 2
        dma("pool", wd_sb[wb][:, 0:nf, :],
            w_fd_d[f0 * 128:(f0 + nf) * 128, :].rearrange("(f p) n -> p f n", p=128),
            r=[], w=["wd%d" % wb])
        for fi in range(nf):
            rg = w_next()
            ru = w_next()
            for tg in range(NTG):
                bg = next_bank()
                fm_matmuls(rg, tg, bg)
                bu = next_bank()
                fm_matmuls(ru, tg, bu)
                sb = si_ % 2
                si_ += 1
                act(sgt[:, sb, :], ps[bg][:, :], AF.Silu, r=["ps%d" % bg], w=["sgt%d" % sb])
                tt_("dve", actT[:, fi, tg * 512:(tg + 1) * 512], sgt[:, sb, :], ps[bu][:, :], ALU.mult,
                    r=["sgt%d" % sb, "ps%d" % bu], w=["actT.%d.%d" % (fi, tg)])
            w_done(2)
        for tt in range(NTT):
            for fh in range(2):
                bank = next_bank()
                for fi in range(nf):
                    mm(ps[bank][:, :], actT[:, fi, tt * 128:(tt + 1) * 128],
                       wd_sb[wb][:, fi, fh * 512:(fh + 1) * 512], fi == 0, fi == nf - 1,
                       r=["actT.%d.%d" % (fi, tt // 4), "wd%d" % wb], w=["ps%d" % bank])
                tt_("dve", x_tm[:, tt, fh * 512:(fh + 1) * 512], x_tm[:, tt, fh * 512:(fh + 1) * 512],
                    ps[bank][:, :], ALU.add, r=["x.%d" % tt, "ps%d" % bank], w=["x.%d" % tt])
            if last and tt >= 1:
                final_norm(tt - 1)
        f0 += nf
    final_norm(NTT - 1)
    return finish()


_NC_CACHE = {}


def _get_nc():
    if "nc" not in _NC_CACHE:
        _NC_CACHE["nc"] = build_program()
    return _NC_CACHE["nc"]


def make_in_maps(inputs):
    x = np.ascontiguousarray(inputs["x"], dtype=np.float32)
    shared = {
        "g_mix_norm": inputs["g_mix_norm"].reshape(1, D),
        "w_in": inputs["w_in"].reshape(D, IN_W),
        "b_in": inputs["b_in"].reshape(1, IN_W),
        "sinks": inputs["sinks"].reshape(1, NQH),
        "conv_w": inputs["conv_w"].reshape(CW, CC),
        "conv_b": inputs["conv_b"].reshape(1, CC),
        "ln_g": inputs["ln_g"].reshape(1, CC),
        "ln_b": inputs["ln_b"].reshape(1, CC),
        "w_attn_proj": inputs["w_attn_proj"].reshape(ATTN_W, D),
        "w_conv_proj": inputs["w_conv_proj"].reshape(CC, D),
        "b_conv_proj": inputs["b_conv_proj"].reshape(1, D),
        "w_out": inputs["w_out"].reshape(D, D),
        "g_ffn_norm": inputs["g_ffn_norm"].reshape(1, D),
        "w_ffn_in": inputs["w_ffn_in"].reshape(D, 2 * DFF),
        "w_ffn_down": inputs["w_ffn_down"].reshape(DFF, D),
        "g_final": inputs["g_final"].reshape(1, D),
    }
    shared = {k: np.ascontiguousarray(np.asarray(v), dtype=np.float32) for k, v in shared.items()}
    maps = []
    for i in range(NCORES):
        m = dict(shared)
        m["x"] = np.ascontiguousarray(x[i])
        maps.append(m)
    return maps


def kernel(**inputs):
    inputs = {k: np.asarray(v) for k, v in inputs.items()}
    nc = _get_nc()
    in_maps = make_in_maps(inputs)
    res = run_bass_kernel_spmd(nc, in_maps, core_ids=list(range(NCORES)))
    out = np.stack([np.asarray(r["out"]).reshape(T, D) for r in res.results], axis=0)
    return out.astype(np.float32)
```

```python
from contextlib import ExitStack

import numpy as np
import concourse.bass as bass
import concourse.mybir as mybir
from concourse.bass_utils import run_bass_kernel_spmd

F32 = mybir.dt.float32
BF16 = mybir.dt.bfloat16
I32 = mybir.dt.int32
AF = mybir.ActivationFunctionType
ALU = mybir.AluOpType

NCORES = 8
D = 1024
T = 2048
NTT = 16
NTG = 4
HD = 64
NQH = 8
ATTN_W = 512
KV_W = 128
CC = 512
CW = 31
Q_OFF = 0
K_OFF = 512
V_OFF = 640
GLU_OFF = 768
GATE_OFF = 1792
IN_W = 3840
DFF = 2816
NFF = 22
EPS = 1e-5

ENGS = ("pe", "act", "dve", "pool", "sp")
NDMA_SEM = 16


class Op:
    __slots__ = ("eng", "fn", "dma", "deps", "signal", "ticket", "dsem",
                 "dticket", "idx", "gidx")

    def __init__(self, eng, fn, dma):
        self.eng = eng
        self.fn = fn
        self.dma = dma
        self.deps = []
        self.signal = False
        self.ticket = 0
        self.dsem = 0
        self.dticket = 0


class Sched:
    def __init__(self):
        self.ops = {e: [] for e in ENGS}
        self.lastw = {}
        self.readers = {}
        self.dma_hist = {e: [] for e in ENGS}
        self.n = 0

    @staticmethod
    def _rkey(op):
        return ("d", op.gidx) if op.dma else ("c", op.eng)

    def add(self, eng, fn, r=(), w=(), dma=False):
        op = Op(eng, fn, dma)
        op.gidx = self.n
        self.n += 1
        op.idx = len(self.ops[eng])
        deps = {}
        for x in r:
            d = self.lastw.get(x)
            if d is not None:
                deps[d] = True
        for x in w:
            d = self.lastw.get(x)
            if d is not None:
                deps.setdefault(d, False)
            for rd in self.readers.get(x, {}).values():
                deps.setdefault(rd, False)
        for d, raw in deps.items():
            if d is op:
                continue
            if (not dma) and (not d.dma) and d.eng == eng:
                if eng == "pe":
                    continue
            op.deps.append(d)
            d.signal = True
        if dma:
            hist = self.dma_hist[eng]
            k = len(hist)
            op.dsem = k % NDMA_SEM
            op.dticket = 16 * (k // NDMA_SEM + 1)
            if k >= NDMA_SEM:
                op.deps.append(hist[k - NDMA_SEM])
            hist.append(op)
        for x in r:
            self.readers.setdefault(x, {})[self._rkey(op)] = op
        for x in w:
            self.lastw[x] = op
            self.readers[x] = {}
        self.ops[eng].append(op)
        return op

    def alias(self, news, olds):
        if isinstance(news, str):
            news = [news]
        for new in news:
            self._alias1(new, olds)

    def _alias1(self, new, olds):
        rd = self.readers.setdefault(new, {})
        for o in olds:
            cands = list(self.readers.get(o, {}).values())
            lw = self.lastw.get(o)
            if lw is not None:
                cands.append(lw)
            for c in cands:
                k = self._rkey(c)
                if k not in rd or rd[k].gidx < c.gidx:
                    rd[k] = c

    def emit(self, nc, stack):
        for e in ENGS:
            t = 0
            for op in self.ops[e]:
                if op.dma:
                    continue
                if op.signal:
                    t += 1
                    op.ticket = t
        csem = {e: stack.enter_context(nc.semaphore("c_" + e)) for e in ENGS}
        dsem = {e: [stack.enter_context(nc.semaphore("d_%s_%d" % (e, i)))
                    for i in range(NDMA_SEM)] for e in ("sp", "pool", "act")}
        block = stack.enter_context(nc.Block())

        def run(ename):
            def body(eng):
                known = {}
                for op in self.ops[ename]:
                    waits = {}
                    for d in op.deps:
                        if d.dma:
                            key, val = ("d", d.eng, d.dsem), d.dticket
                        else:
                            key, val = ("c", d.eng), d.ticket
                        if known.get(key, 0) >= val:
                            continue
                        if waits.get(key, 0) < val:
                            waits[key] = val
                    for key, val in waits.items():
                        sem = csem[key[1]] if key[0] == "c" else dsem[key[1]][key[2]]
                        eng.wait_ge(sem, val)
                        known[key] = val
                    ins = op.fn(eng)
                    if ins is None:
                        continue
                    if op.dma:
                        ins.then_inc(dsem[ename][op.dsem], 16)
                    elif op.signal:
                        ins.then_inc(csem[ename], 1)
            return body

        block.tensor(run("pe"))
        block.scalar(run("act"))
        block.vector(run("dve"))
        block.gpsimd(run("pool"))
        block.sync(run("sp"))


class Arena:
    def __init__(self, t, nbytes):
        self.t = t
        self.nbytes = nbytes

    def f32(self, off, n):
        assert off % 4 == 0 and off + 4 * n <= self.nbytes, (off, n)
        return self.t[:, off // 4: off // 4 + n]

    def bf16(self, off, n):
        assert off % 4 == 0 and n % 2 == 0 and off + 2 * n <= self.nbytes, (off, n)
        return self.t[:, off // 4: off // 4 + n // 2].bitcast(BF16)

    def i32(self, off, n):
        return self.t[:, off // 4: off // 4 + n].bitcast(I32)


def build_program(stop_after="all", dumps=()):
    nc = bass.Bass("TRN2", target_bir_lowering=False)
    S = Sched()
    stack = ExitStack()

    def din(name, shape):
        return nc.dram_tensor(name, list(shape), F32, kind="ExternalInput").ap()

    x_d = din("x", [T, D])
    g_mix_d = din("g_mix_norm", [1, D])
    w_in_d = din("w_in", [D, IN_W])
    b_in_d = din("b_in", [1, IN_W])
    sinks_d = din("sinks", [1, NQH])
    conv_w_d = din("conv_w", [CW, CC])
    conv_b_d = din("conv_b", [1, CC])
    ln_g_d = din("ln_g", [1, CC])
    ln_b_d = din("ln_b", [1, CC])
    w_ap_d = din("w_attn_proj", [ATTN_W, D])
    w_cp_d = din("w_conv_proj", [CC, D])
    b_cp_d = din("b_conv_proj", [1, D])
    w_out_d = din("w_out", [D, D])
    g_ffn_d = din("g_ffn_norm", [1, D])
    w_fi_d = din("w_ffn_in", [D, 2 * DFF])
    w_fd_d = din("w_ffn_down", [DFF, D])
    g_fin_d = din("g_final", [1, D])
    out_d = nc.dram_tensor("out", [T, D], F32, kind="ExternalOutput").ap()

    ARENA_BYTES = 211968
    arena_t = stack.enter_context(nc.sbuf_tensor("arena", [128, ARENA_BYTES // 4], F32))
    A = Arena(arena_t, ARENA_BYTES)
    off = 0

    def region(nbytes):
        nonlocal off
        o = off
        off += (nbytes + 31) // 32 * 32
        assert off <= ARENA_BYTES, off
        return o

    X_OFF = region(65536)
    GBC_OFF = region(4096)
    IDENT_OFF = region(256)
    MASKC_OFF = region(256)
    MASKP_OFF = region(256)
    ONES_OFF = region(256)
    MB_OFF = region(2048)
    BIN_OFF = region(30 * 4)
    BQP_OFF = region(4 * 4)
    BVBC_OFF = region(128 * 4)
    CONVW_OFF = region(4 * CW * 4)
    CVEC_OFF = region(3 * 4 * 4)
    BCP_OFF = region(8 * 4)
    SINK_OFF = region(8 * 4)
    SS_OFF = region(48 * 4)
    RSTD_OFF = region(48 * 4)
    IOTA_OFF = region(128 * 4)
    EPSB_OFF = region(32)
    EM_OFF = region(64)
    WCOL_OFF = region(16 * 8 * 4)
    H_OFF = region(32768)
    M_OFF = region(41280)
    WOUT_OFF = region(16384)
    TMP_OFF = region(28736)
    NRING = 8
    RING_OFF = region(2048 * NRING)

    x_tm = A.f32(X_OFF, 16 * 1024).rearrange("p (n d) -> p n d", n=16)
    attnT = A.bf16(X_OFF, 4 * T).rearrange("p (c t) -> p c t", c=4)
    convT = A.bf16(X_OFF + 16384, 4 * T).rearrange("p (c t) -> p c t", c=4)
    wS = A.bf16(X_OFF + 32768, 16 * 8 * 32).rearrange("p (g m c) -> p g m c", g=16, m=8)
    ZSW = 544
    zs = [A.bf16(X_OFF + 40960 + 8704 * i, 2 * 4 * ZSW).rearrange("p (h g t) -> p h g t", h=2, g=4)
          for i in range(2)]
    xstage = A.f32(X_OFF + 49152, 4 * 1024).rearrange("p (s d) -> p s d", s=4)

    gbc = A.f32(GBC_OFF, 1024)
    ident = A.bf16(IDENT_OFF, 128)
    mask_c = A.bf16(MASKC_OFF, 128)
    mask_p = A.bf16(MASKP_OFF, 128)
    ones_bf = A.bf16(ONES_OFF, 128)
    maskb = A.bf16(MB_OFF, 1024).rearrange("p (w h q) -> p w h q", w=2, h=4)
    b_in_fm = A.f32(BIN_OFF, 30)
    bq_perm = A.f32(BQP_OFF, 4)
    bv_bc = A.f32(BVBC_OFF, 128)
    convw = A.f32(CONVW_OFF, 4 * CW).rearrange("p (c j) -> p c j", c=4)
    cvec = A.f32(CVEC_OFF, 12).rearrange("p (v c) -> p v c", v=3)
    bcp = A.f32(BCP_OFF, 8)
    expsink = A.f32(SINK_OFF, 8)
    ss = A.f32(SS_OFF, 48)
    rstd = A.f32(RSTD_OFF, 48)
    iota_i = A.i32(IOTA_OFF, 128)
    epsb = A.f32(EPSB_OFF, 1)
    emask = A.bf16(EM_OFF, 32)
    wcol = A.f32(WCOL_OFF, 128).rearrange("p (g m) -> p g m", g=16)
    hT = A.bf16(H_OFF, 8 * T).rearrange("p (c t) -> p c t", c=8)
    qT = A.bf16(M_OFF, 4 * T).rearrange("p (c t) -> p c t", c=4)
    kT = A.bf16(M_OFF + 16384, T)
    v_aug = A.bf16(M_OFF + 20480, 16 * 2 * 65).rearrange("p (n g d) -> p n g d", n=16, g=2)
    ZPAD = 32
    zT = A.bf16(M_OFF + 24640, 4 * (T + ZPAD)).rearrange("p (c t) -> p c t", c=4)
    mergedT = A.bf16(M_OFF, 8 * T).rearrange("p (c t) -> p c t", c=8)
    w_out_sb = A.bf16(WOUT_OFF, 8 * 1024).rearrange("p (k n) -> p k n", k=8)
    ring = [A.bf16(RING_OFF + 2048 * i, 1024).rearrange("p (k m) -> p k m", k=8)
            for i in range(NRING)]
    NXN = 3
    xn = A.bf16(TMP_OFF, NXN * 1024).rearrange("p (b d) -> p b d", b=NXN)
    NPT = 3
    ptb = A.bf16(TMP_OFF, NPT * 2 * 512).rearrange("p (s w q) -> p s w q", s=NPT, w=2)
    attn_tm = A.bf16(TMP_OFF + 6144, 2 * 512).rearrange("p (b f) -> p b f", b=2)
    den = A.f32(TMP_OFF + 8192, 2 * 8).rearrange("p (b h) -> p b h", b=2)
    y32 = A.f32(TMP_OFF + 8256, 4 * 512).rearrange("p (c t) -> p c t", c=4)
    sig = A.f32(TMP_OFF + 8256, 2 * 512).rearrange("p (b t) -> p b t", b=2)
    ybf = A.bf16(TMP_OFF + 16448, 4 * 512).rearrange("p (c t) -> p c t", c=4)
    ysq = A.bf16(TMP_OFF + 20544, 4 * 512).rearrange("p (c t) -> p c t", c=4)
    w32 = arena_t[0:32, (TMP_OFF + 20544) // 4:(TMP_OFF + 20544) // 4 + 512]
    w32b = arena_t[0:32, (TMP_OFF + 16448) // 4:(TMP_OFF + 16448) // 4 + 256].bitcast(BF16)
    stA = A.f32(TMP_OFF + 24640, 512)
    stB = A.f32(TMP_OFF + 26688, 512)
    etmp = A.f32(TMP_OFF + 8256, 8 * 512).rearrange("p (b k t) -> p b k t", b=2, k=4)
    FF_GROUPS = [5, 5, 6, 6]
    actT = A.bf16(M_OFF, 6 * T).rearrange("p (f t) -> p f t", f=6)
    wd_sb = [A.bf16(M_OFF + 24576 + 12288 * i, 6 * 1024).rearrange("p (f n) -> p f n", f=6)
             for i in range(2)]
    sgt = A.f32(M_OFF + 49152, 2 * 512).rearrange("p (b t) -> p b t", b=2)

    ps = [stack.enter_context(nc.psum_tensor("ps%d" % i, [128, 512], F32)) for i in range(8)]
    ps_bf = [p.bitcast(BF16) for p in ps]
    bank_ctr = [0]

    def next_bank():
        b = bank_ctr[0] % 8
        bank_ctr[0] += 1
        return b

    def dma(eng, out, in_, r=(), w=(), **kw):
        return S.add(eng, lambda e: e.dma_start(out=out, in_=in_, **kw), r=r, w=w, dma=True)

    def op(eng, fn, r=(), w=()):
        return S.add(eng, fn, r=r, w=w)

    def mm(out, lhsT, rhs, start, stop, r, w):
        return op("pe", lambda e: e.matmul(out=out, lhsT=lhsT, rhs=rhs, start=start, stop=stop), r, w)

    def act(out, in_, func, r, w, **kw):
        return op("act", lambda e: e.activation(out=out, in_=in_, func=func, **kw), r, w)

    def tt_(eng, out, in0, in1, alu, r, w):
        return op(eng, lambda e: e.tensor_tensor(out=out, in0=in0, in1=in1, op=alu), r, w)

    def ts_(eng, out, in0, s1, s2, op0, op1, r, w):
        if s2 is None:
            return op(eng, lambda e: e.tensor_scalar(out=out, in0=in0, scalar1=s1, scalar2=None,
                                                     op0=op0), r, w)
        return op(eng, lambda e: e.tensor_scalar(out=out, in0=in0, scalar1=s1, scalar2=s2,
                                                 op0=op0, op1=op1), r, w)

    def stt_(eng, out, in0, scalar, in1, op0, op1, r, w):
        return op(eng, lambda e: e.scalar_tensor_tensor(out=out, in0=in0, scalar=scalar, in1=in1,
                                                        op0=op0, op1=op1), r, w)

    def hres(tg):
        return ["hT.%d" % t for t in range(4 * tg, 4 * tg + 4)]

    dump_list = []
    dumps_avail = {}

    def finish():
        for nm in dumps:
            ap, res = dumps_avail[nm]
            d = nc.dram_tensor("dbg_" + nm, list(ap.shape), ap.dtype, kind="ExternalOutput").ap()
            dma("sp", d, ap, r=res, w=["dbg_" + nm])
            dump_list.append("dbg_" + nm)
        S.add("sp", lambda e: None, r=["out.%d" % t for t in range(NTT)] + dump_list)
        S.emit(nc, stack)
        stack.close()
        return nc

    wtasks = []
    wstate = {"issued": 0, "consumed": 0, "done": 0}

    def w_issue_upto(n):
        while wstate["issued"] < min(n, len(wtasks)):
            i = wstate["issued"]
            wtasks[i](i % NRING)
            wstate["issued"] += 1

    def w_next():
        i = wstate["consumed"]
        wstate["consumed"] += 1
        assert i < wstate["issued"], (i, wstate)
        return i % NRING

    def w_done(n=1):
        wstate["done"] += n
        w_issue_upto(wstate["done"] + NRING)

    def wtask_cols(src, col_blocks):
        def t(r):
            for i_, (m0, c0, wd) in enumerate(col_blocks):
                wl = ["ring%d.%d" % (r, i_)] + (["ring%d.1" % r] if len(col_blocks) == 1 else [])
                dma("pool", ring[r][:, :, m0:m0 + wd],
                    src[:, c0:c0 + wd].rearrange("(k p) m -> p k m", p=128),
                    w=wl)
        return t

    def wtask_proj(c):
        def t(r):
            dma("pool", ring[r][:, 0:4, :],
                w_ap_d[:, c * 128:(c + 1) * 128].rearrange("(k p) m -> p k m", p=128),
                w=["ring%d.0" % r])
            dma("pool", ring[r][:, 4:8, :],
                w_cp_d[:, c * 128:(c + 1) * 128].rearrange("(k p) m -> p k m", p=128),
                w=["ring%d.1" % r])
        return t

    for c in range(4):
        wtasks.append(wtask_cols(w_in_d, [(0, Q_OFF + 64 * c, 64), (64, Q_OFF + 64 * (c + 4), 64)]))
    wtasks.append(wtask_cols(w_in_d, [(0, K_OFF, 128)]))
    wtasks.append(wtask_cols(w_in_d, [(0, V_OFF, 128)]))
    for c in range(4):
        wtasks.append(wtask_cols(w_in_d, [(0, GLU_OFF + 128 * c, 128)]))
        wtasks.append(wtask_cols(w_in_d, [(0, GLU_OFF + CC + 128 * c, 128)]))
    for c in range(8):
        wtasks.append(wtask_proj(c))
        wtasks.append(wtask_cols(w_in_d, [(0, GATE_OFF + 128 * c, 128)]))
        wtasks.append(wtask_cols(w_in_d, [(0, GATE_OFF + D + 128 * c, 128)]))
    for f in range(NFF):
        wtasks.append(wtask_cols(w_fi_d, [(0, 128 * f, 128)]))
        wtasks.append(wtask_cols(w_fi_d, [(0, DFF + 128 * f, 128)]))

    NCD = dict(allow_slow_non_contiguous=True)
    xv = x_d.rearrange("(n p) d -> p n d", p=128)
    dma("sp", gbc, g_mix_d.broadcast_to([128, D]), w=["gbc"])
    for tt in range(4):
        dma("sp", xstage[:, tt, :], xv[:, tt, :], w=["xs%d" % tt])
    dma("sp", b_in_fm, b_in_d.rearrange("o (c p) -> p (o c)", p=128), w=["b_in_fm"], **NCD)
    for g in range(2):
        dma("sp", bq_perm[64 * g:64 * (g + 1), :],
            b_in_d[:, 256 * g:256 * (g + 1)].rearrange("o (c p) -> p (o c)", p=64),
            w=["bq_perm%d" % g], **NCD)
    dma("sp", bv_bc, b_in_d[:, V_OFF:V_OFF + 128].broadcast_to([128, 128]), w=["bv_bc"])
    dma("sp", expsink, sinks_d.broadcast_to([128, NQH]), w=["expsink"])

    op("pool", lambda e: e.memset(ss, 0.0), w=["ss"])
    op("pool", lambda e: e.memset(epsb, EPS), w=["epsb"])
    op("pool", lambda e: e.iota(iota_i, [[1, 128]], base=0, channel_multiplier=-1), w=["iota"])
    ts_("pool", ident, iota_i, 0, None, ALU.is_equal, None, r=["iota"], w=["ident"])
    ts_("pool", mask_c, iota_i, 0, None, ALU.is_ge, None, r=["iota"], w=["mask_c"])
    ts_("pool", mask_p, iota_i, 0, None, ALU.is_lt, None, r=["iota"], w=["mask_p"])
    op("pool", lambda e: e.memset(ones_bf, 1.0 / CC), w=["ones"])
    op("pool", lambda e: e.memset(w32[0:1, :], 0.0), w=["w32z"])

    w_issue_upto(NRING)

    def late_consts():
        dma("sp", w32[1:32, :], conv_w_d, r=[], w=["w32"])
        dma("sp", cvec[:, 0, :], conv_b_d.rearrange("o (c p) -> p (o c)", p=128), w=["cvec0"], **NCD)
        dma("sp", cvec[:, 1, :], ln_g_d.rearrange("o (c p) -> p (o c)", p=128), w=["cvec1"], **NCD)
        dma("sp", cvec[:, 2, :], ln_b_d.rearrange("o (c p) -> p (o c)", p=128), w=["cvec2"], **NCD)
        dma("sp", bcp, b_cp_d.rearrange("o (c p) -> p (o c)", p=128), w=["bcp"], **NCD)

    def rms_stage1a(src_tile, tt, si, src_res):
        b = tt % NXN
        col = 16 * si + tt
        sres = "ss%d" % col
        rres = "rstd%d" % col
        act(xn[:, b, :], src_tile, AF.Square, r=[src_res, "ss"], w=["xn%d" % b, sres],
            accum_out=ss[:, col:col + 1])
        act(rstd[:, col:col + 1], ss[:, col:col + 1], AF.Sqrt, r=[sres, "epsb"], w=[rres],
            scale=1.0 / D, bias=epsb)

    def rms_stage1b(src_tile, tt, si, src_res):
        b = tt % NXN
        col = 16 * si + tt
        rres = "rstd%d" % col
        op("dve", lambda e: e.reciprocal(out=rstd[:, col:col + 1], in_=rstd[:, col:col + 1]),
           r=[rres], w=[rres])
        stt_("dve", xn[:, b, :], src_tile, rstd[:, col:col + 1], gbc, ALU.mult, ALU.mult,
             r=[src_res, rres, "gbc"], w=["xn%d" % b])

    def rms_stage1(src_tile, tt, si, src_res):
        rms_stage1a(src_tile, tt, si, src_res)
        rms_stage1b(src_tile, tt, si, src_res)

    def rms_stage2(tt, dstT, dst_pref, copy_eng="act"):
        b = tt % NXN
        bank = next_bank()
        for c in range(8):
            op("pe", lambda e, c=c: e.transpose(out=ps_bf[bank][:, c * 128:(c + 1) * 128],
                                                in_=xn[:, b, c * 128:(c + 1) * 128],
                                                identity=ident),
               r=["xn%d" % b, "ident"], w=["ps%d" % bank])
        if copy_eng == "act":
            act(dstT[:, :, tt * 128:(tt + 1) * 128],
                ps_bf[bank].rearrange("p (c j) -> p c j", c=8), AF.Copy,
                r=["ps%d" % bank], w=["%s.%d" % (dst_pref, tt)])
        else:
            op("dve", lambda e: e.tensor_copy(out=dstT[:, :, tt * 128:(tt + 1) * 128],
                                              in_=ps_bf[bank].rearrange("p (c j) -> p c j", c=8)),
               r=["ps%d" % bank], w=["%s.%d" % (dst_pref, tt)])

    SCALE = HD ** -0.5

    def fm_matmuls(r, tg, bank):
        for k in range(8):
            mm(ps[bank][:, :], ring[r][:, k, :], hT[:, k, tg * 512:(tg + 1) * 512],
               k == 0, k == 7, r=["ring%d.0" % r, "ring%d.1" % r] + hres(tg), w=["ps%d" % bank])

    op("pool", lambda e: e.memset(v_aug[:, :, :, 64:65], 1.0), w=["v_ones"])
    op("pool", lambda e: e.memset(zT[:, :, 0:ZPAD], 0.0), w=["z_pad"])
    rq = [w_next() for _ in range(4)]
    rk = w_next()
    rv = w_next()

    def part1(tg):
        for c in range(4):
            bank = next_bank()
            fm_matmuls(rq[c], tg, bank)
            act(qT[:, c, tg * 512:(tg + 1) * 512], ps[bank][:, :], AF.Identity,
                r=["ps%d" % bank, "bq_perm0", "bq_perm1"], w=["qT.%d" % tg], bias=bq_perm[:, c:c + 1])
        bank = next_bank()
        fm_matmuls(rk, tg, bank)
        act(kT[:, tg * 512:(tg + 1) * 512], ps[bank][:, :], AF.Identity,
            r=["ps%d" % bank, "b_in_fm"], w=["kT.%d" % tg], bias=b_in_fm[:, 4:5])
        bank = next_bank()
        for j in range(4):
            tt = 4 * tg + j
            for k in range(8):
                mm(ps[bank][:, j * 128:(j + 1) * 128], hT[:, k, tt * 128:(tt + 1) * 128],
                   ring[rv][:, k, :], k == 0, k == 7,
                   r=["ring%d.0" % rv, "ring%d.1" % rv, "hT.%d" % tt], w=["ps%d" % bank])
        tt_("dve", v_aug[:, 4 * tg:4 * tg + 4, :, 0:64],
            ps[bank].rearrange("p (n g d) -> p n g d", n=4, g=2),
            bv_bc.rearrange("p (g d) -> p g d", g=2).unsqueeze(1).broadcast_to([128, 4, 2, 64]),
            ALU.add, r=["ps%d" % bank, "bv_bc"], w=["v.%d" % tg])

    for tt in range(NTT):
        st = tt % 4
        if tt >= 4:
            dma("sp", xstage[:, st, :], xv[:, tt, :], w=["xs%d" % st])
        if tt == 15:
            late_consts()
        rms_stage1(xstage[:, st, :], tt, 0, "xs%d" % st)
        if tt >= 1:
            rms_stage2(tt - 1, hT, "hT", copy_eng="dve")
        if tt >= 4 and tt % 4 == 1:
            part1(tt // 4 - 1)
    rms_stage2(NTT - 1, hT, "hT", copy_eng="dve")
    part1(3)
    w_done(6)
    act(expsink, expsink, AF.Exp, r=["expsink"], w=["expsink"])
    NEGB = 30000.0
    ts_("dve", maskb[:, 1, :, :], mask_c.unsqueeze(1).broadcast_to([128, 4, 128]), -1.0, NEGB,
        ALU.add, ALU.mult, r=["mask_c"], w=["maskb1"])
    ts_("dve", maskb[:, 0, :, :], mask_p.unsqueeze(1).broadcast_to([128, 4, 128]), -1.0, NEGB,
        ALU.add, ALU.mult, r=["mask_p"], w=["maskb0"])

    op("dve", lambda e: e.tensor_copy(out=w32b, in_=w32), r=["w32", "w32z"], w=["w32b"])
    wbank = next_bank()
    selv = ident[0:32, 0:32].rearrange("p (m q) -> p m q", q=4)
    for G in range(16):
        for jj in range(4):
            op("pe", lambda e, G=G, jj=jj: e.matmul(
                out=ps[wbank][32 * jj:32 * (jj + 1), 8 * G:8 * (G + 1)],
                lhsT=w32b[:, 32 * G:32 * (G + 1)], rhs=selv[:, :, jj],
                start=True, stop=True, tile_position=(0, 32 * jj)),
               r=["w32b", "ident"], w=["ps%d" % wbank])
    op("dve", lambda e: e.tensor_copy(out=wcol, in_=ps[wbank][:, 0:128].rearrange("p (g m) -> p g m", g=16)),
       r=["ps%d" % wbank], w=["wcol"])

    dumps_avail["hT"] = (hT, ["hT.%d" % t for t in range(NTT)])
    if stop_after == "A":
        return finish()

    S.alias(["wS"] + ["zs%d.%d.%d" % (h_, g_, j_) for h_ in range(2) for g_ in range(4) for j_ in range(4)],
            ["xs%d" % i for i in range(4)])
    tt_("dve", emask, ident[:, 0:32], ident[:, 32:64], ALU.add, r=["ident"], w=["emask"])
    tt_("dve", emask, emask, ident[:, 64:96], ALU.add, r=["ident", "emask"], w=["emask"])
    tt_("dve", emask, emask, ident[:, 96:128], ALU.add, r=["ident", "emask"], w=["emask"])

    def build_diag(c):
        for G in range(4 * c, 4 * c + 4):
            tt_("dve", wS[:, G, :, :], emask.unsqueeze(1).broadcast_to([128, 8, 32]),
                wcol[:, G, :].unsqueeze(2).broadcast_to([128, 8, 32]), ALU.mult,
                r=["emask", "wcol"], w=["wS"])

    def fill_zs(tg, half):
        t0 = tg * 512 + (ZPAD - 31)
        for cg in range(4):
            for jj in range(4):
                zr = ["z_pad"] + ["zT.%d.%d" % (c_, t_) for c_ in (2 * half, 2 * half + 1)
                                  for t_ in ((tg - 1, tg) if tg > 0 else (tg,))]
                dma("sp" if (cg < 2 or tg == 0) else "pool", zs[half][32 * jj:32 * (jj + 1), :, cg, 0:540],
                    zT[32 * cg:32 * (cg + 1), 2 * half:2 * half + 2, t0 + jj:t0 + jj + 540],
                    r=zr, w=["zs%d.%d.%d" % (half, cg, jj)])

    for c in range(4):
        ra = w_next()
        rb = w_next()
        for tg in range(NTG):
            ba = next_bank()
            fm_matmuls(ra, tg, ba)
            bb = next_bank()
            fm_matmuls(rb, tg, bb)
            sb = (c * NTG + tg) % 2
            act(sig[:, sb, :], ps[bb][:, :], AF.Sigmoid, r=["ps%d" % bb, "b_in_fm"], w=["sig%d" % sb],
                bias=b_in_fm[:, 10 + c:11 + c])
            stt_("dve", zT[:, c, ZPAD + tg * 512:ZPAD + (tg + 1) * 512], ps[ba][:, :],
                 b_in_fm[:, 6 + c:7 + c], sig[:, sb, :], ALU.add, ALU.mult,
                 r=["ps%d" % ba, "sig%d" % sb, "b_in_fm"], w=["zT.%d.%d" % (c, tg)])
        w_done(2)
        build_diag(c)

    dumps_avail["qT"] = (qT, ["qT.%d" % t for t in range(NTG)])
    dumps_avail["kT"] = (kT, ["kT.%d" % t for t in range(NTG)])
    dumps_avail["v_aug"] = (v_aug, ["v.%d" % t for t in range(NTG)] + ["v_ones"])
    dumps_avail["zT"] = (zT, ["zT.%d.%d" % (c, t) for c in range(4) for t in range(NTG)] + ["z_pad"])
    if stop_after == "B":
        return finish()

    for hh in range(2):
        dma("pool", w_out_sb[:, :, hh * 512:(hh + 1) * 512],
            w_out_d[:, hh * 512:(hh + 1) * 512].rearrange("(k p) n -> p k n", p=128),
            w=["w_out%d" % hh])

    S.alias(["pt.%d.%d" % (s_, w_) for s_ in range(NPT) for w_ in range(2)], ["xn0", "xn1", "xn2"])
    S.alias(["y32.%d" % c_ for c_ in range(4)], ["sig0", "sig1"])
    S.alias(["ybf.%d" % c_ for c_ in range(4)] + ["ysq.%d" % c_ for c_ in range(4)], ["w32", "w32z", "w32b"])
    steps = [(n, g) for n in range(NTT) for g in range(2)]

    def emit_scores(si):
        n, g = steps[si]
        s = si % NPT
        rows = slice(64 * g, 64 * g + 64)
        q_rhs = qT[rows, :, n * 128:(n + 1) * 128]
        qres = ["qT.%d" % (n // 4)]
        blocks = [(1, n)] + ([(0, n - 1)] if n > 0 else [])
        for (wsel, kb) in blocks:
            bank = next_bank()
            mm(ps[bank][:, :], kT[rows, kb * 128:(kb + 1) * 128], q_rhs, True, False,
               r=qres + ["kT.%d" % (kb // 4)], w=["ps%d" % bank])
            mm(ps[bank][:, :], ident, maskb[:, wsel, :, :].rearrange("p h q -> p (h q)"), False, True,
               r=["ident", "maskb%d" % wsel], w=["ps%d" % bank])
            pres = "pt.%d.%d" % (s, wsel)
            act(ptb[:, s, wsel, :], ps[bank][:, :], AF.Exp, r=["ps%d" % bank], w=[pres],
                scale=SCALE)

    def emit_pv(si):
        n, g = steps[si]
        s = si % NPT
        b = n % 2
        bank = next_bank()
        for i in range(4):
            o = ps[bank][:, i * 65:(i + 1) * 65]
            if n > 0:
                mm(o, ptb[:, s, 0, i * 128:(i + 1) * 128], v_aug[:, n - 1, g, :], True, False,
                   r=["pt.%d.0" % s, "v.%d" % ((n - 1) // 4), "v_ones"], w=["ps%d" % bank])
            mm(o, ptb[:, s, 1, i * 128:(i + 1) * 128], v_aug[:, n, g, :], n == 0, True,
               r=["pt.%d.1" % s, "v.%d" % (n // 4), "v_ones"], w=["ps%d" % bank])
        pso = ps[bank][:, 0:260].rearrange("p (h d) -> p h d", h=4)
        dres = "den%d.%d" % (b, g)
        tt_("dve", den[:, b, 4 * g:4 * g + 4], pso[:, :, 64], expsink[:, 4 * g:4 * g + 4], ALU.add,
            r=["ps%d" % bank, "expsink"], w=[dres])
        op("dve", lambda e: e.reciprocal(out=den[:, b, 4 * g:4 * g + 4], in_=den[:, b, 4 * g:4 * g + 4]),
           r=[dres], w=[dres])
        tt_("dve", attn_tm[:, b, 256 * g:256 * (g + 1)].rearrange("p (h d) -> p h d", h=4),
            pso[:, :, 0:64],
            den[:, b, 4 * g:4 * g + 4].unsqueeze(2).broadcast_to([128, 4, 64]), ALU.mult,
            r=["ps%d" % bank, dres], w=["attn_tm%d.%d" % (b, g)])

    def emit_attn_T(n):
        b = n % 2
        bank = next_bank()
        for c in range(4):
            op("pe", lambda e, c=c: e.transpose(out=ps_bf[bank][:, c * 128:(c + 1) * 128],
                                                in_=attn_tm[:, b, c * 128:(c + 1) * 128],
                                                identity=ident),
               r=["attn_tm%d.0" % b, "attn_tm%d.1" % b, "ident"], w=["ps%d" % bank])
        act(attnT[:, :, n * 128:(n + 1) * 128],
            ps_bf[bank][:, 0:512].rearrange("p (c j) -> p c j", c=4), AF.Copy,
            r=["ps%d" % bank], w=["attnT.%d" % n])

    conv_units = [(tg, c) for tg in range(NTG) for c in range(4)]

    def emit_conv_unit(ui):
        tg, c = conv_units[ui]
        bank = next_bank()
        half = c // 2
        for m in range(8):
            for cg in range(4):
                op("pe", lambda e, m=m, cg=cg: e.matmul(
                    out=ps[bank][32 * cg:32 * (cg + 1), :], lhsT=wS[:, 4 * c + cg, m, :],
                    rhs=zs[half][:, c % 2, cg, 4 * m:4 * m + 512],
                    start=(m == 0), stop=(m == 7), tile_position=(0, 32 * cg)),
                   r=["wS"] + ["zs%d.%d.%d" % (half, cg, j_) for j_ in range(4)], w=["ps%d" % bank])
        act(y32[:, c, :], ps[bank][:, :], AF.Identity, r=["ps%d" % bank, "cvec0", "cvec1", "cvec2"],
            w=["y32.%d" % c], bias=cvec[:, 0, c:c + 1])
        act(ysq[:, c, :], ps[bank][:, :], AF.Square, r=["ps%d" % bank, "cvec0", "cvec1", "cvec2"],
            w=["ysq.%d" % c], bias=cvec[:, 0, c:c + 1])
        op("dve", lambda e: e.tensor_copy(out=ybf[:, c, :], in_=y32[:, c, :]),
           r=["y32.%d" % c], w=["ybf.%d" % c])

    def ln_a(tg):
        bm = next_bank()
        for c in range(4):
            mm(ps[bm][:, :], ones_bf, ybf[:, c, :], c == 0, c == 3,
               r=["ones", "ybf.%d" % c], w=["ps%d" % bm])
        bq = next_bank()
        for c in range(4):
            mm(ps[bq][:, :], ones_bf, ysq[:, c, :], c == 0, c == 3,
               r=["ones", "ysq.%d" % c], w=["ps%d" % bq])
        act(stA, ps[bm][:, :], AF.Square, r=["ps%d" % bm], w=["stA"])
        tt_("dve", stB, ps[bq][:, :], stA, ALU.subtract, r=["ps%d" % bq, "stA"], w=["stB"])
        act(stB, stB, AF.Ln, r=["stB", "epsb"], w=["stB"], bias=epsb)
        act(stB, stB, AF.Exp, r=["stB"], w=["stB"], scale=-0.5)
        tt_("dve", stA, ps[bm][:, :], stB, ALU.mult, r=["ps%d" % bm, "stB"], w=["stA"])

    def ln_b(tg, c):
        tt_("dve", y32[:, c, :], y32[:, c, :], stB, ALU.mult, r=["y32.%d" % c, "stB"],
            w=["y32.%d" % c])
        tt_("dve", y32[:, c, :], y32[:, c, :], stA, ALU.subtract, r=["y32.%d" % c, "stA"],
            w=["y32.%d" % c])
        act(convT[:, c, tg * 512:(tg + 1) * 512], y32[:, c, :], AF.Silu,
            r=["y32.%d" % c, "cvec0", "cvec1", "cvec2"], w=["convT.%d.%d" % (c, tg)],
            scale=cvec[:, 1, c:c + 1], bias=cvec[:, 2, c:c + 1])

    fill_zs(0, 0)
    fill_zs(0, 1)
    LOOK = NPT - 1
    for si in range(min(LOOK, len(steps))):
        emit_scores(si)
    cu = 0
    for si in range(len(steps)):
        if si + LOOK < len(steps):
            emit_scores(si + LOOK)
        emit_pv(si)
        if steps[si][1] == 1:
            n = steps[si][0]
            if n >= 4 and n % 4 == 0:
                for c_ in range(4):
                    ln_b(n // 4 - 1, c_)
            emit_conv_unit(cu)
            cu += 1
            if n % 4 == 1 and n // 4 + 1 < NTG:
                fill_zs(n // 4 + 1, 0)
            if n % 4 == 3 and n // 4 + 1 < NTG:
                fill_zs(n // 4 + 1, 1)
            if n % 4 == 3:
                ln_a(n // 4)
            if n >= 1:
                emit_attn_T(n - 1)
    emit_attn_T(NTT - 1)
    for c in range(4):
        ln_b(NTG - 1, c)
    assert cu == len(conv_units)

    dumps_avail["attnT"] = (attnT, ["attnT.%d" % n for n in range(NTT)])
    dumps_avail["convT"] = (convT, ["convT.%d.%d" % (c, t) for c in range(4) for t in range(NTG)])
    if stop_after == "D":
        return finish()

    S.alias(["mergedT.%d.%d" % (c_, t_) for c_ in range(8) for t_ in range(NTG)], ["qT.%d" % t for t in range(NTG)] + ["kT.%d" % t for t in range(NTG)]
            + ["v.%d" % t for t in range(NTG)] + ["v_ones", "z_pad"]
            + ["zT.%d.%d" % (c, t) for c in range(4) for t in range(NTG)])
    S.alias(["etmp.%d.%d" % (b_, i_) for b_ in range(2) for i_ in range(4)], ["y32.0", "y32.1", "y32.2", "y32.3", "ybf.0", "ybf.1", "ybf.2", "ybf.3",
                     "ysq.0", "ysq.1", "ysq.2", "ysq.3"])
    attn_res = [["attnT.%d" % n for n in range(4 * tg, 4 * tg + 4)] for tg in range(NTG)]
    ei = 0
    for c in range(8):
        rp = w_next()
        rg0 = w_next()
        rg1 = w_next()
        for tg in range(NTG):
            tsl = slice(tg * 512, (tg + 1) * 512)
            eb = ei % 2
            ei += 1
            bya = next_bank()
            for k in range(4):
                mm(ps[bya][:, :], ring[rp][:, k, :], attnT[:, k, tsl], k == 0, k == 3,
                   r=["ring%d.0" % rp, "ring%d.1" % rp] + attn_res[tg], w=["ps%d" % bya])
            byc = next_bank()
            for k in range(4):
                mm(ps[byc][:, :], ring[rp][:, 4 + k, :], convT[:, k, tsl], k == 0, k == 3,
                   r=["ring%d.0" % rp, "ring%d.1" % rp] + ["convT.%d.%d" % (k, tg)], w=["ps%d" % byc])
            bg0 = next_bank()
            fm_matmuls(rg0, tg, bg0)
            bg1 = next_bank()
            fm_matmuls(rg1, tg, bg1)
            er = ["etmp.%d.%d" % (eb, i) for i in range(4)]
            act(etmp[:, eb, 0, :], ps[bg0][:, :], AF.Sigmoid, r=["ps%d" % bg0, "b_in_fm"],
                w=[er[0]], bias=b_in_fm[:, 14 + c:15 + c])
            act(etmp[:, eb, 1, :], ps[bg1][:, :], AF.Sigmoid, r=["ps%d" % bg1, "b_in_fm"],
                w=[er[1]], bias=b_in_fm[:, 22 + c:23 + c])
            tt_("dve", etmp[:, eb, 2, :], ps[bya][:, :], etmp[:, eb, 0, :], ALU.mult,
                r=["ps%d" % bya, er[0]], w=[er[2]])
            stt_("dve", etmp[:, eb, 3, :], ps[byc][:, :], bcp[:, c:c + 1], etmp[:, eb, 1, :],
                 ALU.add, ALU.mult, r=["ps%d" % byc, er[1], "bcp"], w=[er[3]])
            tt_("dve", mergedT[:, c, tsl], etmp[:, eb, 2, :], etmp[:, eb, 3, :], ALU.add,
                r=[er[2], er[3]], w=["mergedT.%d.%d" % (c, tg)])
        w_done(3)

    dumps_avail["mergedT"] = (mergedT, ["mergedT.%d.%d" % (c, t) for c in range(8) for t in range(NTG)])
    if stop_after == "E":
        return finish()

    S.alias(["x.%d" % t_ for t_ in range(NTT)], ["attnT.%d" % n for n in range(NTT)]
            + ["convT.%d.%d" % (c, t) for c in range(4) for t in range(NTG)] + ["wS"]
            + ["zs%d.%d.%d" % (h_, g_, j_) for h_ in range(2) for g_ in range(4) for j_ in range(4)])
    S.alias("xn0", ["pt.%d.%d" % (s, w) for s in range(NPT) for w in range(2)])
    S.alias("xn1", ["pt.%d.%d" % (s, w) for s in range(NPT) for w in range(2)])
    S.alias("xn2", ["pt.%d.%d" % (s, w) for s in range(NPT) for w in range(2)])
    dma("sp", gbc, g_ffn_d.broadcast_to([128, D]), w=["gbc"])
    for tt in range(NTT):
        dma("sp", x_tm[:, tt, :], xv[:, tt, :], w=["x.%d" % tt])
    for tt in range(NTT):
        for fh in range(2):
            bank = next_bank()
            for k in range(8):
                mm(ps[bank][:, :], mergedT[:, k, tt * 128:(tt + 1) * 128],
                   w_out_sb[:, k, fh * 512:(fh + 1) * 512], k == 0, k == 7,
                   r=["mergedT.%d.%d" % (k, tt // 4), "w_out%d" % fh], w=["ps%d" % bank])
            tt_("dve", x_tm[:, tt, fh * 512:(fh + 1) * 512], x_tm[:, tt, fh * 512:(fh + 1) * 512],
                ps[bank][:, :], ALU.add, r=["x.%d" % tt, "ps%d" % bank], w=["x.%d" % tt])
        rms_stage1a(x_tm[:, tt, :], tt, 1, "x.%d" % tt)
        if tt >= 1:
            rms_stage1b(x_tm[:, tt - 1, :], tt - 1, 1, "x.%d" % (tt - 1))
        if tt >= 2:
            rms_stage2(tt - 2, hT, "hT")
    rms_stage1b(x_tm[:, NTT - 1, :], NTT - 1, 1, "x.%d" % (NTT - 1))
    rms_stage2(NTT - 2, hT, "hT")
    rms_stage2(NTT - 1, hT, "hT")

    dumps_avail["x_tm"] = (x_tm, ["x.%d" % t for t in range(NTT)])
    if stop_after == "F":
        return finish()

    ffn_old = (["w_out0", "w_out1"]
               + ["mergedT.%d.%d" % (c, t) for c in range(8) for t in range(NTG)])
    S.alias(["actT.%d.%d" % (f_, t_) for f_ in range(6) for t_ in range(NTG)]
            + ["wd0", "wd1", "sgt0", "sgt1"], ffn_old)
    ov = out_d.rearrange("(n p) d -> p n d", p=128)

    def final_norm(tt):
        col = 32 + tt
        sres = "ss%d" % col
        rres = "rstd%d" % col
        act(xn[:, 0, :], x_tm[:, tt, :], AF.Square, r=["x.%d" % tt, "ss"], w=["xn0", sres],
            accum_out=ss[:, col:col + 1])
        act(rstd[:, col:col + 1], ss[:, col:col + 1], AF.Sqrt, r=[sres, "epsb"], w=[rres],
            scale=1.0 / D, bias=epsb)
        op("dve", lambda e: e.reciprocal(out=rstd[:, col:col + 1], in_=rstd[:, col:col + 1]),
           r=[rres], w=[rres])
        stt_("dve", x_tm[:, tt, :], x_tm[:, tt, :], rstd[:, col:col + 1], gbc, ALU.mult, ALU.mult,
             r=["x.%d" % tt, rres, "gbc"], w=["x.%d" % tt])
        dma("sp", ov[:, tt, :], x_tm[:, tt, :], r=["x.%d" % tt], w=["out.%d" % tt])

    f0 = 0
    si_ = 0
    for gi, nf in enumerate(FF_GROUPS):
        last = gi == len(FF_GROUPS) - 1
        if last:
            dma("sp", gbc, g_fin_d.broadcast_to([128, D]), w=["gbc"])
        wb = gi % 2
        dma("pool", wd_sb[wb][:, 0:nf, :],
            w_fd_d[f0 * 128:(f0 + nf) * 128, :].rearrange("(f p) n -> p f n", p=128),
            r=[], w=["wd%d" % wb])
        for fi in range(nf):
            rg = w_next()
            ru = w_next()
            for tg in range(NTG):
                bg = next_bank()
                fm_matmuls(rg, tg, bg)
                bu = next_bank()
                fm_matmuls(ru, tg, bu)
                sb = si_ % 2
                si_ += 1
                act(sgt[:, sb, :], ps[bg][:, :], AF.Silu, r=["ps%d" % bg], w=["sgt%d" % sb])
                tt_("dve", actT[:, fi, tg * 512:(tg + 1) * 512], sgt[:, sb, :], ps[bu][:, :], ALU.mult,
                    r=["sgt%d" % sb, "ps%d" % bu], w=["actT.%d.%d" % (fi, tg)])
            w_done(2)
        for tt in range(NTT):
            for fh in range(2):
                bank = next_bank()
                for fi in range(nf):
                    mm(ps[bank][:, :], actT[:, fi, tt * 128:(tt + 1) * 128],
                       wd_sb[wb][:, fi, fh * 512:(fh + 1) * 512], fi == 0, fi == nf - 1,
                       r=["actT.%d.%d" % (fi, tt // 4), "wd%d" % wb], w=["ps%d" % bank])
                tt_("dve", x_tm[:, tt, fh * 512:(fh + 1) * 512], x_tm[:, tt, fh * 512:(fh + 1) * 512],
                    ps[bank][:, :], ALU.add, r=["x.%d" % tt, "ps%d" % bank], w=["x.%d" % tt])
            if last and tt >= 1:
                final_norm(tt - 1)
        f0 += nf
    final_norm(NTT - 1)
    return finish()


_NC_CACHE = {}


def _get_nc():
    if "nc" not in _NC_CACHE:
        _NC_CACHE["nc"] = build_program()
    return _NC_CACHE["nc"]


def make_in_maps(inputs):
    x = np.ascontiguousarray(inputs["x"], dtype=np.float32)
    shared = {
        "g_mix_norm": inputs["g_mix_norm"].reshape(1, D),
        "w_in": inputs["w_in"].reshape(D, IN_W),
        "b_in": inputs["b_in"].reshape(1, IN_W),
        "sinks": inputs["sinks"].reshape(1, NQH),
        "conv_w": inputs["conv_w"].reshape(CW, CC),
        "conv_b": inputs["conv_b"].reshape(1, CC),
        "ln_g": inputs["ln_g"].reshape(1, CC),
        "ln_b": inputs["ln_b"].reshape(1, CC),
        "w_attn_proj": inputs["w_attn_proj"].reshape(ATTN_W, D),
        "w_conv_proj": inputs["w_conv_proj"].reshape(CC, D),
        "b_conv_proj": inputs["b_conv_proj"].reshape(1, D),
        "w_out": inputs["w_out"].reshape(D, D),
        "g_ffn_norm": inputs["g_ffn_norm"].reshape(1, D),
        "w_ffn_in": inputs["w_ffn_in"].reshape(D, 2 * DFF),
        "w_ffn_down": inputs["w_ffn_down"].reshape(DFF, D),
        "g_final": inputs["g_final"].reshape(1, D),
    }
    shared = {k: np.ascontiguousarray(np.asarray(v), dtype=np.float32) for k, v in shared.items()}
    maps = []
    for i in range(NCORES):
        m = dict(shared)
        m["x"] = np.ascontiguousarray(x[i])
        maps.append(m)
    return maps


def kernel(**inputs):
    inputs = {k: np.asarray(v) for k, v in inputs.items()}
    nc = _get_nc()
    in_maps = make_in_maps(inputs)
    res = run_bass_kernel_spmd(nc, in_maps, core_ids=list(range(NCORES)))
    out = np.stack([np.asarray(r["out"]).reshape(T, D) for r in res.results], axis=0)
    return out.astype(np.float32)
```
